# Optimizing a Trainium2 kernel written in Bass

```python
import math, functools
import jax, jax.numpy as jnp
from jax import lax
import numpy as np

D_MODEL = 2048
BATCH = 2
SEQ = 4096
DEPTH = 1
DEC_BATCH = 32
DEC_SEQ = 4
PAST_LEN = 8192
PAGE_SIZE = 128

HEAD_DIM = 64
GLA_HEADS = 10
GLA_DK = 64
GLA_DV = 128
GLA_RANK = 16
GLA_TAU = 16.0
GLA_CHUNK = 64
DIL_HEADS = 12
DIL_PAIRS = ((128, 1), (512, 4), (2048, 16))
DIL_WINDOW = 2048
QBLK = 128
ROPE_THETA = 10000.0
D_FF = 5504
CONV_W = 3
EPS = 1e-6

GLA_QK = GLA_HEADS * GLA_DK
GLA_VW = GLA_HEADS * GLA_DV
DIL_W = DIL_HEADS * HEAD_DIM
MIX_WIDTH = GLA_VW + DIL_W
SPLIT_SIZES = (GLA_QK, GLA_QK, GLA_VW, GLA_VW, GLA_RANK, DIL_W, DIL_W, DIL_W)
N_IN = GLA_QK * 2 + GLA_VW * 2 + GLA_RANK + DIL_W * 3

kernel_name = 'hymba_gla_dilated_convffn_step'


def rmsnorm(x, g):
    xf = x.astype(jnp.float32)
    y = xf * lax.rsqrt(jnp.mean(xf * xf, axis=-1, keepdims=True) + EPS)
    return (y * g.astype(jnp.float32)).astype(x.dtype)


def split_in_proj(proj):
    parts, start = [], 0
    for size in SPLIT_SIZES:
        parts.append(proj[..., start:start + size])
        start += size
    return parts


def rope(x, pos):
    half = HEAD_DIM // 2
    inv_freq = ROPE_THETA ** (-2.0 * jnp.arange(half, dtype=jnp.float32) / HEAD_DIM)
    ang = pos.astype(jnp.float32)[:, None] * inv_freq[None, :]
    cos = jnp.cos(ang)[None, :, None, :]
    sin = jnp.sin(ang)[None, :, None, :]
    xf = x.astype(jnp.float32)
    x1, x2 = xf[..., :half], xf[..., half:]
    return jnp.concatenate([x1 * cos - x2 * sin, x2 * cos + x1 * sin], axis=-1).astype(x.dtype)


def gla_recurrence(q, k, v, log_a, s0):
    b_, t_, h_, dk = q.shape
    dv = v.shape[-1]
    c = math.gcd(t_, GLA_CHUNK)
    n = t_ // c

    def chunks(a):
        return a.reshape(b_, n, c, h_, a.shape[-1]).transpose(1, 0, 3, 2, 4)

    causal = jnp.tril(jnp.ones((c, c), dtype=bool))[:, :, None]

    def step(s, inp):
        qc, kc, vc, gc = inp
        cum = jnp.cumsum(gc, axis=2)
        diff = cum[:, :, :, None, :] - cum[:, :, None, :, :]
        decay = jnp.exp(jnp.where(causal, diff, -jnp.inf))
        scores = jnp.einsum('bhik,bhjk,bhijk->bhij', qc, kc, decay)
        o = (jnp.einsum('bhij,bhjv->bhiv', scores, vc)
             + jnp.einsum('bhik,bhkv->bhiv', qc * jnp.exp(cum), s))
        last = cum[:, :, -1:, :]
        s_new = (jnp.exp(last[:, :, 0, :])[..., None] * s
                 + jnp.einsum('bhjk,bhjv->bhkv', kc * jnp.exp(last - cum), vc))
        return s_new, o

    s_fin, o = lax.scan(step, s0, (chunks(q), chunks(k), chunks(v), chunks(log_a)))
    o = o.transpose(1, 0, 3, 2, 4).reshape(b_, t_, h_, dv)
    return o, s_fin


def dilated_branch(q, k_all, v_all, q_idx, window, dilation):
    n_keys = window // dilation + 1
    key_idx = q_idx[:, None] - dilation * jnp.arange(n_keys)[None, :]
    valid = key_idx >= 0
    key_idx = jnp.maximum(key_idx, 0)
    kg = jnp.take(k_all, key_idx, axis=1)
    vg = jnp.take(v_all, key_idx, axis=1)
    s = jnp.einsum('bthd,btnhd->bthn', q, kg, preferred_element_type=jnp.float32) * (HEAD_DIM ** -0.5)
    s = jnp.where(valid[None, :, None, :], s, -jnp.inf)
    m = jnp.max(s, axis=-1, keepdims=True)
    p = jnp.exp(s - m)
    den = jnp.sum(p, axis=-1, keepdims=True)
    num = jnp.einsum('bthn,btnhd->bthd', p, vg.astype(jnp.float32))
    return num, den, m


def dilated_mixture(q, k_all, v_all, q_idx):
    parts = [dilated_branch(q, k_all, v_all, q_idx, w, r) for (w, r) in DIL_PAIRS]
    m_all = functools.reduce(jnp.maximum, [m for _, _, m in parts])
    num = functools.reduce(jnp.add, [nm * jnp.exp(m - m_all) for nm, _, m in parts])
    den = functools.reduce(jnp.add, [d * jnp.exp(m - m_all) for _, d, m in parts])
    return (num / den).astype(q.dtype)


def dilated_attention(q, k_all, v_all, q_idx):
    b_, t_, h_, d_ = q.shape
    if t_ % QBLK != 0 or t_ <= QBLK:
        return dilated_mixture(q, k_all, v_all, q_idx)
    nb = t_ // QBLK
    qb = q.reshape(b_, nb, QBLK, h_, d_).transpose(1, 0, 2, 3, 4)
    ib = q_idx.reshape(nb, QBLK)
    ob = lax.map(lambda a: dilated_mixture(a[0], k_all, v_all, a[1]), (qb, ib))
    return ob.transpose(1, 0, 2, 3, 4).reshape(b_, t_, h_, d_)


def mixer_block(h, pos, gla_s0, k_hist, v_hist, w_in, w_gate_up, b_gate, gla_norm, w_out):
    b_, t_, _ = h.shape
    f32 = jnp.float32
    q_a, k_a, v_a, r_a, z_a, q_b, k_b, v_b = split_in_proj(h @ w_in)
    log_a = jax.nn.log_sigmoid((z_a @ w_gate_up + b_gate).astype(f32)) / GLA_TAU
    o_a, s_fin = gla_recurrence(
        q_a.reshape(b_, t_, GLA_HEADS, GLA_DK).astype(f32) * (GLA_DK ** -0.5),
        k_a.reshape(b_, t_, GLA_HEADS, GLA_DK).astype(f32),
        v_a.reshape(b_, t_, GLA_HEADS, GLA_DV).astype(f32),
        log_a.reshape(b_, t_, GLA_HEADS, GLA_DK),
        gla_s0.astype(f32))
    gate = jax.nn.silu(r_a.reshape(b_, t_, GLA_HEADS, GLA_DV).astype(f32))
    o_a = (rmsnorm(o_a, gla_norm) * gate).astype(h.dtype).reshape(b_, t_, GLA_VW)
    q = rope(q_b.reshape(b_, t_, DIL_HEADS, HEAD_DIM), pos)
    k = rope(k_b.reshape(b_, t_, DIL_HEADS, HEAD_DIM), pos)
    v = v_b.reshape(b_, t_, DIL_HEADS, HEAD_DIM)
    k_all = jnp.concatenate([k_hist.astype(k.dtype), k], axis=1)
    v_all = jnp.concatenate([v_hist.astype(v.dtype), v], axis=1)
    hist = k_hist.shape[1]
    o_b = dilated_attention(q, k_all, v_all, hist + jnp.arange(t_)).reshape(b_, t_, DIL_W)
    buf = hist if hist > 0 else min(DIL_WINDOW, t_)
    y = jnp.concatenate([o_a, o_b], axis=-1) @ w_out
    start = k_all.shape[1] - buf
    return y, s_fin.astype(h.dtype), k_all[:, start:], v_all[:, start:]


def conv_ffn(h, conv_hist, w_up, conv_w, conv_b, w_down):
    t_ = h.shape[1]
    up = h @ w_up
    full = jnp.concatenate([conv_hist.astype(up.dtype), up], axis=1)
    conv = conv_b + functools.reduce(jnp.add, [conv_w[j] * full[:, j:j + t_] for j in range(CONV_W)])
    u, g = conv[..., :D_FF], conv[..., D_FF:]
    return (jax.nn.silu(g) * u) @ w_down, full[:, full.shape[1] - (CONV_W - 1):]


def setup_inputs(seed: int = 0) -> dict:
    key = jax.random.key(seed)
    ks = jax.random.split(key, 18)
    f32 = jnp.float32
    win_buf = min(DIL_WINDOW, PAST_LEN)

    def nrm(k, shape, scale):
        return jax.random.normal(k, shape, f32) * scale

    return {
        'x_prompt': nrm(ks[0], (BATCH, SEQ, D_MODEL), 1.0),
        'x_sample': nrm(ks[1], (DEC_BATCH, DEC_SEQ, D_MODEL), 1.0),
        'state_gla': nrm(ks[2], (DEPTH, DEC_BATCH, GLA_HEADS, GLA_DK, GLA_DV), 1.0),
        'cache_dil_k': nrm(ks[3], (DEPTH, DEC_BATCH, win_buf, DIL_HEADS, HEAD_DIM), 1.0),
        'cache_dil_v': nrm(ks[4], (DEPTH, DEC_BATCH, win_buf, DIL_HEADS, HEAD_DIM), 1.0),
        'state_ffn_conv': nrm(ks[5], (DEPTH, DEC_BATCH, CONV_W - 1, 2 * D_FF), 1.0),
        'norm_mix': 1.0 + nrm(ks[6], (DEPTH, D_MODEL), 0.02),
        'w_in': nrm(ks[7], (DEPTH, D_MODEL, N_IN), D_MODEL ** -0.5),
        'w_gate_up': nrm(ks[8], (DEPTH, GLA_RANK, GLA_QK), GLA_RANK ** -0.5),
        'b_gate': nrm(ks[9], (DEPTH, GLA_QK), 0.1),
        'gla_norm': 1.0 + nrm(ks[10], (DEPTH, GLA_DV), 0.02),
        'w_out': nrm(ks[11], (DEPTH, MIX_WIDTH, D_MODEL), MIX_WIDTH ** -0.5),
        'norm_ffn': 1.0 + nrm(ks[12], (DEPTH, D_MODEL), 0.02),
        'w_ffn_up': nrm(ks[13], (DEPTH, D_MODEL, 2 * D_FF), D_MODEL ** -0.5),
        'ffn_conv_w': nrm(ks[14], (DEPTH, CONV_W, 2 * D_FF), CONV_W ** -0.5),
        'ffn_conv_b': nrm(ks[15], (DEPTH, 2 * D_FF), 0.02),
        'w_ffn_down': nrm(ks[16], (DEPTH, D_FF, D_MODEL), D_FF ** -0.5),
        'norm_final': 1.0 + nrm(ks[17], (D_MODEL,), 0.02),
    }


def reference(x_prompt, x_sample, state_gla, cache_dil_k, cache_dil_v, state_ffn_conv,
              norm_mix, w_in, w_gate_up, b_gate, gla_norm, w_out,
              norm_ffn, w_ffn_up, ffn_conv_w, ffn_conv_b, w_ffn_down, norm_final):
    bp, tp, _ = x_prompt.shape
    ts = x_sample.shape[1]
    pos_p = jnp.arange(tp)
    pos_s = PAST_LEN + jnp.arange(ts)
    hp, hs = x_prompt, x_sample
    gla_p, gla_s, kp_l, ks_l, vp_l, vs_l, cp_l, cs_l = [], [], [], [], [], [], [], []
    for l in range(DEPTH):
        zero_gla = jnp.zeros((bp, GLA_HEADS, GLA_DK, GLA_DV), jnp.float32)
        empty_kv = jnp.zeros((bp, 0, DIL_HEADS, HEAD_DIM), x_prompt.dtype)
        a_p, sg_p, k_p, v_p = mixer_block(rmsnorm(hp, norm_mix[l]), pos_p, zero_gla, empty_kv, empty_kv,
                                          w_in[l], w_gate_up[l], b_gate[l], gla_norm[l], w_out[l])
        a_s, sg_s, k_s, v_s = mixer_block(rmsnorm(hs, norm_mix[l]), pos_s, state_gla[l], cache_dil_k[l],
                                          cache_dil_v[l], w_in[l], w_gate_up[l], b_gate[l], gla_norm[l], w_out[l])
        hp = hp + a_p
        hs = hs + a_s
        zero_conv = jnp.zeros((bp, CONV_W - 1, 2 * D_FF), x_prompt.dtype)
        f_p, c_p = conv_ffn(rmsnorm(hp, norm_ffn[l]), zero_conv,
                            w_ffn_up[l], ffn_conv_w[l], ffn_conv_b[l], w_ffn_down[l])
        f_s, c_s = conv_ffn(rmsnorm(hs, norm_ffn[l]), state_ffn_conv[l],
                            w_ffn_up[l], ffn_conv_w[l], ffn_conv_b[l], w_ffn_down[l])
        hp = hp + f_p
        hs = hs + f_s
        gla_p.append(sg_p); gla_s.append(sg_s)
        kp_l.append(k_p); ks_l.append(k_s)
        vp_l.append(v_p); vs_l.append(v_s)
        cp_l.append(c_p); cs_l.append(c_s)
    y_prompt = rmsnorm(hp, norm_final)
    y_sample = rmsnorm(hs, norm_final)
    new_gla_p = jnp.stack(gla_p, 0)
    new_gla_s = jnp.stack(gla_s, 0)
    new_k_p = jnp.stack(kp_l, 0)
    new_k_s = jnp.stack(ks_l, 0)
    new_v_p = jnp.stack(vp_l, 0)
    new_v_s = jnp.stack(vs_l, 0)
    new_conv_p = jnp.stack(cp_l, 0)
    new_conv_s = jnp.stack(cs_l, 0)
    return (y_prompt, y_sample, new_gla_p, new_gla_s, new_k_p, new_k_s, new_v_p, new_v_s, new_conv_p, new_conv_s)
```

```python
import numpy as np

from contextlib import ExitStack
import concourse.bass as bass
import concourse.mybir as mybir
from concourse.bass_utils import run_bass_kernel_spmd

F32 = mybir.dt.float32
BF16 = mybir.dt.bfloat16
ALU = mybir.AluOpType
AF = mybir.ActivationFunctionType
AX = mybir.AxisListType
ENGS = ("pe", "act", "dve", "pool", "sp")
EPS = 1e-6
NT = 34
T_SMP = 32
NKB = 26
D_FF = 5504
NF = 43


class Res:
    __slots__ = ("name", "w", "r")

    def __init__(self, name=""):
        self.name = name
        self.w = {}
        self.r = {}


class Sched:
    def __init__(self, nc, stack, n_dma_sems=32):
        self.nc = nc
        self.q = {e: [] for e in ENGS}
        self.cnt = {e: 0 for e in ENGS}
        self.waited = {e: {} for e in ENGS}
        self.sem = {}
        for e in ENGS:
            self.sem[e] = stack.enter_context(nc.semaphore("s_" + e))
        self.ndma = n_dma_sems
        self.dtot = [0] * n_dma_sems
        for k in range(n_dma_sems):
            self.sem[("d", k)] = stack.enter_context(nc.semaphore("s_d%d" % k))
        self.dnext = 0
        self.dnext_sw = 0

    def _deps(self, eng, reads, writes, extra=None):
        d = {}

        def add(k, v):
            if v > d.get(k, 0):
                d[k] = v
        for r in reads:
            for k, v in r.w.items():
                add(k, v)
        for w in writes:
            for k, v in w.w.items():
                add(k, v)
            for k, v in w.r.items():
                add(k, v)
        if extra:
            for k, v in extra:
                add(k, v)
        wl = []
        wd = self.waited[eng]
        for k, v in d.items():
            if v > wd.get(k, 0):
                wd[k] = v
                wl.append((k, v))
        return wl

    def _mark(self, tok, reads, writes, joins):
        for r in reads:
            if tok[1] > r.r.get(tok[0], 0):
                r.r[tok[0]] = tok[1]
        for w in writes:
            w.w = {tok[0]: tok[1]}
            w.r = {}
        for w in joins:
            w.w[tok[0]] = max(w.w.get(tok[0], 0), tok[1])

    def emit(self, eng, fn, reads=(), writes=(), signal=True, joins=(), extra=None):
        wl = self._deps(eng, reads, list(writes), extra)
        if joins:
            d2 = {}
            for w in joins:
                for k, v in w.r.items():
                    if v > d2.get(k, 0):
                        d2[k] = v
            wd = self.waited[eng]
            for k, v in d2.items():
                if v > wd.get(k, 0):
                    wd[k] = v
                    wl.append((k, v))
        if signal:
            self.cnt[eng] += 1
            tok = (eng, self.cnt[eng])
        else:
            tok = (eng, self.cnt[eng] + 1)
        self.q[eng].append((wl, fn, 1 if signal else 0, None))
        self._mark(tok, reads, writes, joins)
        return tok

    def dma(self, eng, fn, reads=(), writes=(), joins=()):
        half = self.ndma // 2
        if eng == "pool":
            k = half + (self.dnext_sw % (self.ndma - half))
            self.dnext_sw += 1
        else:
            k = self.dnext % half
            self.dnext += 1
        key = ("d", k)
        extra = [(key, self.dtot[k])] if self.dtot[k] > 0 else None
        wl = self._deps(eng, reads, list(writes), extra)
        if joins:
            wd = self.waited[eng]
            for w in joins:
                for kk, v in w.r.items():
                    if v > wd.get(kk, 0):
                        wd[kk] = v
                        wl.append((kk, v))
        self.dtot[k] += 16
        tok = (key, self.dtot[k])
        self.q[eng].append((wl, fn, 16, key))
        self._mark(tok, reads, writes, joins)
        return tok

    def barrier(self):
        tot = [(("d", k), self.dtot[k]) for k in range(self.ndma) if self.dtot[k] > 0]
        tot += [(e, self.cnt[e]) for e in ENGS if self.cnt[e] > 0]
        for e in ENGS:
            wl = []
            wd = self.waited[e]
            for k, v in tot:
                if v > wd.get(k, 0):
                    wd[k] = v
                    wl.append((k, v))
            if wl:
                self.q[e].append((wl, None, 0, None))

    def replay(self):
        nc = self.nc
        with nc.Block() as block:
            deco = {"pe": block.tensor, "act": block.scalar, "dve": block.vector,
                    "pool": block.gpsimd, "sp": block.sync}
            for e in ENGS:
                items = self.q[e]
                if not items:
                    continue

                def body(engine, items=items, e=e):
                    for wl, fn, inc, key in items:
                        for k, v in wl:
                            engine.wait_ge(self.sem[k], v)
                        if fn is None:
                            continue
                        ins = fn(engine)
                        if inc == 1:
                            ins.then_inc(self.sem[e], 1)
                        elif inc == 16:
                            ins.then_inc(self.sem[key], 16)
                deco[e](body)
        self.q = {e: [] for e in ENGS}


class T:
    def __init__(self, nc, stack, name, shape, dtype, psum=False):
        if psum:
            self.t = stack.enter_context(nc.psum_tensor("t_" + name, shape, dtype))
        else:
            self.t = stack.enter_context(nc.sbuf_tensor("t_" + name, shape, dtype))
        self.r = Res(name)
        self.psum = psum

    def __getitem__(self, k):
        return self.t[k]


class DT:
    def __init__(self, ap, name=""):
        self.ap = ap
        self.r = Res(name)

    def __getitem__(self, k):
        return self.ap[k]


def _cmult(d):
    d = np.asarray(d)
    ok = d >= 0
    return ((ok & (d <= 128)).astype(np.float32)
            + (ok & (d <= 512) & (d % 4 == 0)).astype(np.float32)
            + (ok & (d <= 2048) & (d % 16 == 0)).astype(np.float32))


def _host_consts(c):
    s = c % 4
    start = 1024 * s
    half = 32
    inv_freq = (np.float32(10000.0) ** (-2.0 * np.arange(half, dtype=np.float32) / 64)).astype(np.float32)
    pos = np.zeros(NT * 128, np.float32)
    for t in range(32):
        pos[t * 128:(t + 1) * 128] = start - 3072 + 128 * t + np.arange(128)
    for p in range(16):
        pos[T_SMP * 128 + p] = 8192 + p % 4
    ang = pos[:, None].astype(np.float32) * inv_freq[None, :]
    cs = np.concatenate([np.cos(ang), np.sin(ang)], axis=1).astype(np.float32)
    flags = np.zeros((128, 32), np.float32)
    for kb in range(25):
        t = kb + 7
        flags[:, kb] = 1.0 if (start - 3072 + 128 * t) >= 0 else 0.0
    flags[:, 25] = 1.0
    flags[:, 26] = 1.0 if s > 0 else 0.0
    p = np.arange(128)[:, None]
    q = np.arange(128)[None, :]
    mask_p = np.zeros((128, 17 * 128), np.float32)
    for i in range(17):
        dl = 16 - i
        mask_p[:, i * 128:(i + 1) * 128] = _cmult(128 * dl + q - p)
    mask_s = np.zeros((128, 4 * 16 * 16 + 16), np.float32)
    qc = np.arange(16)[None, :]
    for sq in range(4):
        for blk in range(16):
            j = 128 * blk + p
            m = _cmult(2048 + (qc % 4) - j) * ((qc // 4) == sq)
            mask_s[:, (sq * 16 + blk) * 16:(sq * 16 + blk) * 16 + 16] = m
    pn = np.arange(128)[:, None]
    mn = _cmult((qc % 4) - (pn % 4)) * ((qc // 4) == (pn // 4)) * (pn < 16)
    mask_s[:, 1024:1040] = mn
    gm = np.zeros((128, 776), np.float32)
    same = (p // 64) == (q // 64)
    gm[:, 0:128] = np.where(same & (p <= q), -1.0 / 16, 0.0)
    gm[:, 128:256] = np.where(same & (p > q), -1.0 / 16, 0.0)
    gm[:, 256:384] = np.where(same & (p <= q), 1.0, 0.0)
    sames = ((p // 4) == (q // 4)) & (p < 16) & (q < 16)
    gm[:, 384:512] = np.where(sames & (p <= q), -1.0 / 16, 0.0)
    gm[:, 512:640] = np.where(sames & (p > q), -1.0 / 16, 0.0)
    gm[:, 640:768] = np.where(sames & (p <= q), 1.0, 0.0)
    for ch in range(2):
        gm[:, 768 + ch] = np.where((np.arange(128) // 64) == ch, -1.0 / 16, 0.0)
    for sq in range(4):
        gm[:, 772 + sq] = np.where((np.arange(128) < 16) & ((np.arange(128) // 4) == sq), -1.0 / 16, 0.0)
    smask = np.zeros((128, 68), np.float32)
    for sq in range(4):
        smask[:, sq * 16:(sq + 1) * 16] = ((np.arange(16) // 4) == sq).astype(np.float32)[None, :]
        smask[:, 64 + sq] = ((np.arange(128) < 16) & ((np.arange(128) // 4) == sq)).astype(np.float32)
    return dict(cs=cs, flags=flags, mask_p=mask_p, mask_s=mask_s, gm=gm, smask=smask)


def build(stop_after="Z"):
    nc = bass.Bass("TRN2", target_bir_lowering=False)

    def din(name, shape):
        return nc.dram_tensor(name, list(shape), F32, kind="ExternalInput").ap()

    def dout(name, shape):
        return DT(nc.dram_tensor(name, list(shape), F32, kind="ExternalOutput").ap(), name)

    def dscr(name, shape, dt):
        return DT(nc.dram_tensor(name, list(shape), dt).ap(), name)

    xall = din("xall", [NT * 128, 2048])
    cs_d = din("cs", [NT * 128, 64])
    flags_d = din("flags", [128, 32])
    maskp_d = din("mask_p", [128, 17 * 128])
    masks_d = din("mask_s", [128, 1040])
    gm_d = din("gm", [128, 776])
    smask_d = din("smask", [128, 68])
    sgla_d = din("sgla", [4, 640, 128])
    ck_d = din("ck", [4, 2048, 768])
    cv_d = din("cv", [4, 2048, 768])
    convp_d = din("convp", [12, 11008])
    nmix_d = din("norm_mix", [1, 2048])
    win_d = din("w_in", [2048, 6160])
    wg_d = din("wg", [16, 640])
    bg_d = din("bg", [1, 640])
    gn_d = din("gnorm", [1, 128])
    wout_d = din("w_out", [2048, 2048])
    nffn_d = din("norm_ffn", [1, 2048])
    if stop_after not in ("P", "A"):
        wup_d = din("w_up", [2048, 11008])
        wdn_d = din("w_down", [5504, 2048])
    nfin_d = din("norm_final", [1, 2048])

    y_main = dout("y_main", [1024, 2048])
    y_smp = dout("y_smp", [16, 2048])
    gla_end = dout("gla_end", [128, 640])
    gla_smp = dout("gla_smp", [4, 128, 640])
    k_main = dout("k_main", [1024, 768])
    v_main = dout("v_main", [1024, 768])
    ks_out = dout("ks_out", [4, 2048, 768])
    vs_out = dout("vs_out", [4, 2048, 768])
    conv_dev = dout("conv_dev", [18, 11008])

    xTs = [dscr("xTs%d" % i, [128, 2048], BF16) for i in range(10)]
    g_kT = [dscr("g_kT%d" % i, [128, 640], BF16) for i in range(10)]
    g_v = [dscr("g_v%d" % i, [128, 1280], BF16) for i in range(10)]
    g_S = [dscr("g_S%d" % i, [128, 4 * 640], BF16) for i in range(10)]
    g_eq = [dscr("g_eq%d" % i, [128, 640], F32) for i in range(10)]
    g_qT = [dscr("g_qT%d" % i, [128, 640], BF16) for i in range(10)]
    g_gate = [dscr("g_gate%d" % i, [128, 1280], BF16) for i in range(10)]
    g_QT = [dscr("g_QT%d" % i, [128, 768], BF16) for i in range(10)]
    g_oT = [dscr("g_oT%d" % i, [128, 2048], BF16) for i in range(10)]
    g_h1 = [dscr("g_h1%d" % i, [128, 2048], F32) for i in range(10)]
    g_h2 = [dscr("g_h2%d" % i, [128, 2048], F32) for i in range(10)]
    KTs = [dscr("KTs%d" % i, [128, 768], BF16) for i in range(NKB)]
    Vs = [dscr("Vs%d" % i, [128, 780], BF16) for i in range(NKB)]

    with ExitStack() as G:
        S = Sched(nc, G)

        def E(eng, fn, r=(), w=(), j=(), signal=True):
            rr = [x.r for x in r if not getattr(x, "psum", False)]
            ww = [x.r for x in w] + [x.r for x in r if getattr(x, "psum", False)]
            extra = []
            for x in j:
                if getattr(x, "psum", False):
                    for k, v in x.r.w.items():
                        if k != eng:
                            extra.append((k, v))
            return S.emit(eng, fn, rr, ww, signal=signal, joins=[x.r for x in j], extra=extra or None)

        def D(eng, fn, r=(), w=(), j=()):
            return S.dma(eng, fn, [x.r for x in r], [x.r for x in w], joins=[x.r for x in j])

        pb = [T(nc, G, "pb%d" % i, [128, 512], F32, psum=True) for i in range(6)]
        pbf = [T(nc, G, "pbf%d" % i, [128, 1024], BF16, psum=True) for i in range(2)]
        bi = [0, 0]

        reserved = []

        def bank():
            while True:
                b = pb[bi[0] % 6]
                bi[0] += 1
                if b not in reserved:
                    return b

        def bankbf():
            b = pbf[bi[1] % 2]
            bi[1] += 1
            return b

        ident = T(nc, G, "ident", [128, 128], BF16)
        identf = T(nc, G, "identf", [128, 128], F32)
        gm = T(nc, G, "gm", [128, 776], F32)
        flags = T(nc, G, "flags", [128, 32], F32)
        smask = T(nc, G, "smask", [128, 68], F32)
        cwT = T(nc, G, "cwT", [128, 86, 12], F32)
        Sst = T(nc, G, "Sst", [128, 640], F32)

        for idt in (ident, identf):
            E("pool", lambda e, t=idt: e.memset(t[:], 0.0), w=[idt])
            E("pool", lambda e, t=idt: e.affine_select(out=t[:], in_=t[:], pattern=[[-1, 128]],
                                                       compare_op=ALU.not_equal, fill=1.0, base=0,
                                                       channel_multiplier=1), r=[idt], w=[idt])
        D("sp", lambda e: e.dma_start(out=gm[:], in_=gm_d), w=[gm])
        D("sp", lambda e: e.dma_start(out=flags[:], in_=flags_d), w=[flags])
        D("sp", lambda e: e.dma_start(out=smask[:], in_=smask_d), w=[smask])
        E("dve", lambda e: e.memset(Sst[:], 0.0), w=[Sst])

        def mmgroup(out_ap, outT, pairs, reads, fresh=True, first=True, last=True):
            n = len(pairs)
            for i, (l, r_) in enumerate(pairs):
                fn = (lambda e, l=l, r_=r_, st=(first and i == 0), sp_=(last and i == n - 1):
                      e.matmul(out_ap, lhsT=l, rhs=r_, start=st, stop=sp_))
                if i == 0 and fresh:
                    E("pe", fn, r=reads, w=[outT], signal=(i == n - 1))
                else:
                    E("pe", fn, r=reads, j=[outT], signal=(i == n - 1))

        def transposes(srcs, dst_bank, width, reads, idn):
            n = len(srcs)
            for i, (a, rows) in enumerate(srcs):
                fn = (lambda e, a=a, i=i, rows=rows:
                      e.transpose(out=dst_bank[0:128, i * width:i * width + rows], in_=a, identity=idn[0:rows, 0:rows]))
                if i == 0:
                    E("pe", fn, r=reads + [idn], w=[dst_bank], signal=(i == n - 1))
                else:
                    E("pe", fn, r=reads + [idn], j=[dst_bank], signal=(i == n - 1))

        def rms_rstd(x_ap, junk, ss, rstd, xr, n):
            E("act", lambda e: e.activation(out=junk[:, 0:n], in_=x_ap, func=AF.Square, accum_out=ss[:, 0:1]),
              r=[xr], w=[junk, ss])
            E("act", lambda e: e.activation(out=ss[:, 1:2], in_=ss[:, 0:1], func=AF.Sqrt, scale=1.0 / n, bias=EPS),
              r=[ss], w=[ss])
            E("dve", lambda e: e.reciprocal(out=rstd[:, 0:1], in_=ss[:, 1:2]), r=[ss], w=[rstd])

        with ExitStack() as ph:
            cst = T(nc, ph, "cst", [12, 11008], F32)
            D("sp", lambda e: e.dma_start(out=cst[:], in_=convp_d), w=[cst])
            import os as _os0
            for f0 in range(0, 0 if _os0.environ.get("KSKIP_P") else 86, 4):
                nf = min(4, 86 - f0)
                bk = bank()
                for i in range(nf):
                    f = f0 + i
                    fn = (lambda e, f=f, i=i, bk=bk:
                          e.transpose(out=bk[0:128, i * 12:i * 12 + 12], in_=cst[0:12, f * 128:(f + 1) * 128],
                                      identity=identf[0:12, 0:12]))
                    if i == 0:
                        E("pe", fn, r=[cst, identf], w=[bk], signal=(i == nf - 1))
                    else:
                        E("pe", fn, r=[cst, identf], j=[bk], signal=(i == nf - 1))
                E("act", lambda e, bk=bk, f0=f0, nf=nf: e.copy(
                    out=cwT[:, f0:f0 + nf, :], in_=bk[:, 0:nf * 12].rearrange("p (a b) -> p a b", a=nf)),
                  r=[bk], j=[cwT])
            S.barrier()
            S.replay()

        if stop_after == "P":
            return nc
        win_v = win_d.rearrange("(kc p) c -> p kc c", p=128)
        with ExitStack() as ph:
            WA = T(nc, ph, "WA", [128, 16, 3472], BF16)
            rngs = [(3840, 16, 0), (640, 640, 16), (1280, 1280, 656), (4624, 768, 1936), (5392, 768, 2704)]
            for (c0, w_, off) in rngs:
                for kq in range(4):
                    D("pool", lambda e, c0=c0, w_=w_, off=off, kq=kq: e.dma_start(
                        out=WA[:, 4 * kq:4 * kq + 4, off:off + w_], in_=win_v[:, 4 * kq:4 * kq + 4, c0:c0 + w_]),
                      j=[WA])
            for sq in range(4):
                for hh in range(2):
                    r0, r1 = (0, 1024) if hh == 0 else (1024, 2044)
                    D("pool", lambda e, sq=sq, r0=r0, r1=r1: e.dma_start(out=ks_out.ap[sq, r0:r1, :], in_=ck_d[sq, r0 + 4:r1 + 4, :]), j=[ks_out])
                    D("pool", lambda e, sq=sq, r0=r0, r1=r1: e.dma_start(out=vs_out.ap[sq, r0:r1, :], in_=cv_d[sq, r0 + 4:r1 + 4, :]), j=[vs_out])
            gmix = T(nc, ph, "gmix", [128, 2048], F32)
            D("sp", lambda e: e.dma_start(out=gmix[:], in_=nmix_d.to_broadcast([128, 2048])), w=[gmix])
            bgb = T(nc, ph, "bgb", [128, 640], F32)
            D("sp", lambda e: e.dma_start(out=bgb[:], in_=bg_d.to_broadcast([128, 640])), w=[bgb])
            wgt = T(nc, ph, "wgt", [16, 640], BF16)
            D("pool", lambda e: e.dma_start(out=wgt[:], in_=wg_d), w=[wgt])
            gmb = T(nc, ph, "gmb", [128, 776], BF16)
            E("dve", lambda e: e.tensor_copy(out=gmb[:], in_=gm[:]), r=[gm], w=[gmb])
            spbb = T(nc, ph, "spbb", [128, 640], BF16)
            xt = [T(nc, ph, "xt0", [128, 2048], F32)] * 2
            xn = T(nc, ph, "xn", [128, 2048], BF16)
            junk = xn
            xT = [T(nc, ph, "xT%d" % i, [128, 16, 128], BF16) for i in range(2)]
            ss = [T(nc, ph, "ss%d" % i, [128, 2], F32) for i in range(2)]
            rstd = [T(nc, ph, "rstd%d" % i, [128, 1], F32) for i in range(2)]
            cst_ = [T(nc, ph, "cs%d" % i, [128, 64], F32) for i in range(2)]
            ksb = T(nc, ph, "ksb", [128, 640], F32)
            zsb = T(nc, ph, "zsb", [128, 16], F32)
            zT = T(nc, ph, "zT", [16, 128], BF16)
            usb = T(nc, ph, "usb", [128, 640], F32)
            spb = T(nc, ph, "spb", [128, 640], F32)
            kdec = T(nc, ph, "kdec", [128, 640], F32)
            khat = T(nc, ph, "khat", [128, 640], BF16)
            eqb = kdec
            ekb = usb
            ktl = T(nc, ph, "ktl", [128, 640], BF16)
            kTt = T(nc, ph, "kTt", [128, 640], BF16)
            Dsb = T(nc, ph, "Dsb", [128, 20], F32)
            vbf = [T(nc, ph, "vbf%d" % i, [128, 1280], BF16) for i in range(2)]
            Sbf = T(nc, ph, "Sbf", [128, 4 * 640], BF16)
            kb32 = T(nc, ph, "kb32", [128, 768], F32)
            krot = [T(nc, ph, "krot0", [128, 768], F32)] * 2
            rt = [T(nc, ph, "rt%d" % i, [128, 384], F32) for i in range(4)]
            kbf = T(nc, ph, "kbf", [128, 768], BF16)
            KTt = [T(nc, ph, "KTt%d" % i, [128, 768], BF16) for i in range(2)]
            vsb = [T(nc, ph, "vsb0", [128, 768], F32)] * 2
            vaug = [T(nc, ph, "vaug%d" % i, [128, 12, 65], BF16) for i in range(2)]
            S0q = [kdec] * 2
            S1q = [usb] * 2
            for va in vaug:
                E("pool", lambda e, va=va: e.memset(va[:], 1.0), w=[va])

            tiles = list(range(32)) + [T_SMP]
            import os as _os
            if _os.environ.get("KTILES"):
                tiles = [int(x) for x in _os.environ["KTILES"].split(",")]
            KSEC = int(_os.environ.get("KSEC", "9"))
            KSUB = int(_os.environ.get("KSUB", "9"))
            for it, t in enumerate(tiles):
                b2 = it % 2
                mixer = (t >= 23)
                mi = (t - 23) if t < 32 else 9
                smp = (t == T_SMP)
                halo = (t >= 7)
                kb = (t - 7) if t < 32 else 25
                X = xt[b2]
                D("sp", lambda e, X=X, t=t: e.dma_start(out=X[:], in_=xall[t * 128:(t + 1) * 128, :]), w=[X])
                if halo:
                    CS = cst_[b2]
                    D("sp", lambda e, CS=CS, t=t: e.dma_start(out=CS[:], in_=cs_d[t * 128:(t + 1) * 128, :]), w=[CS])
                rms_rstd(X[:], junk, ss[b2], rstd[b2], X, 2048)
                E("dve", lambda e, X=X, b2=b2: e.scalar_tensor_tensor(
                    out=xn[:], in0=X[:], scalar=rstd[b2][:, 0:1], in1=gmix[:], op0=ALU.mult, op1=ALU.mult),
                  r=[X, rstd[b2], gmix], w=[xn])
                XT = xT[b2]
                for q4 in range(2):
                    bk = bankbf()
                    transposes([(xn[:, (q4 * 8 + i) * 128:(q4 * 8 + i + 1) * 128], 128) for i in range(8)],
                               bk, 128, [xn], ident)
                    eng = "act" if q4 == 0 else "dve"
                    if eng == "act":
                        E("act", lambda e, bk=bk, XT=XT, q4=q4: e.copy(
                            out=XT[:, q4 * 8:q4 * 8 + 8, :], in_=bk[:, :].rearrange("p (a b) -> p a b", a=8)),
                          r=[bk], j=[XT] if q4 else (), w=[XT] if not q4 else ())
                    else:
                        E("dve", lambda e, bk=bk, XT=XT, q4=q4: e.tensor_copy(
                            out=XT[:, q4 * 8:q4 * 8 + 8, :], in_=bk[:, :].rearrange("p (a b) -> p a b", a=8)),
                          r=[bk], j=[XT])
                if mixer:
                    D("sp", lambda e, XT=XT, mi=mi: e.dma_start(
                        out=xTs[mi].ap.rearrange("p (a b) -> p a b", a=16), in_=XT[:]), r=[XT], w=[xTs[mi]])

                def proj(off, width):
                    res = []
                    c = 0
                    while c < width:
                        n = min(512, width - c)
                        bk = bank()
                        mmgroup(bk[:, 0:n], bk,
                                [(XT[:, kc, :], WA[:, kc, off + c:off + c + n]) for kc in range(16)], [XT, WA])
                        res.append((bk, n, c))
                        c += n
                    return res

                if KSEC < 2:
                    continue
                pz = proj(0, 656)
                (b0, n0, _), (b1, n1, _) = pz
                if KSUB < 1:
                    continue
                E("dve", lambda e, b0=b0: e.tensor_copy(out=zsb[:], in_=b0[:, 0:16]), r=[b0], w=[zsb])
                E("act", lambda e, b0=b0: e.copy(out=ksb[:, 0:496], in_=b0[:, 16:512]), r=[b0], w=[ksb])
                E("act", lambda e, b1=b1: e.copy(out=ksb[:, 496:640], in_=b1[:, 0:144]), r=[b1], j=[ksb])
                VB = vbf[b2]
                pv = proj(656, 1280)
                for i, (bk, n, c) in enumerate(pv):
                    if i % 2 == 0:
                        E("act", lambda e, bk=bk, n=n, c=c, VB=VB: e.copy(out=VB[:, c:c + n], in_=bk[:, 0:n]),
                          r=[bk], w=[VB] if i == 0 else (), j=() if i == 0 else [VB])
                    else:
                        E("dve", lambda e, bk=bk, n=n, c=c, VB=VB: e.tensor_copy(out=VB[:, c:c + n], in_=bk[:, 0:n]),
                          r=[bk], j=[VB])
                if mixer:
                    D("sp", lambda e, mi=mi, VB=VB: e.dma_start(out=g_v[mi].ap, in_=VB[:]), r=[VB], w=[g_v[mi]])
                if KSUB < 2:
                    continue
                bk = bank()
                E("pe", lambda e, bk=bk: e.transpose(out=bk[0:16, 0:128], in_=zsb[:, 0:16], identity=identf[:]),
                  r=[zsb, identf], w=[bk])
                E("act", lambda e, bk=bk: e.copy(out=zT[:], in_=bk[0:16, 0:128]), r=[bk], w=[zT])
                ba = bank()
                bb = bank()
                mmgroup(ba[:, 0:512], ba, [(zT[:, :], wgt[:, 0:512])], [zT, wgt])
                mmgroup(bb[:, 0:128], bb, [(zT[:, :], wgt[:, 512:640])], [zT, wgt])
                E("dve", lambda e, ba=ba: e.tensor_tensor(out=usb[:, 0:512], in0=ba[:, 0:512], in1=bgb[:, 0:512], op=ALU.add),
                  r=[ba, bgb], w=[usb])
                E("dve", lambda e, bb=bb: e.tensor_tensor(out=usb[:, 512:640], in0=bb[:, 0:128], in1=bgb[:, 512:640], op=ALU.add),
                  r=[bb, bgb], j=[usb])
                E("act", lambda e: e.activation(out=usb[:], in_=usb[:], func=AF.Exp, scale=-1.0), r=[usb], w=[usb])
                E("act", lambda e: e.activation(out=spb[:], in_=usb[:], func=AF.Ln, bias=1.0), r=[usb], w=[spb])
                E("dve", lambda e: e.tensor_copy(out=spbb[:], in_=spb[:]), r=[spb], w=[spbb])
                if halo:
                    VS = vsb[b2]
                    pk = proj(1936, 768)
                    for i, (bk, n, c) in enumerate(pk):
                        E("act", lambda e, bk=bk, n=n, c=c: e.copy(out=kb32[:, c:c + n], in_=bk[:, 0:n]),
                          r=[bk], w=[kb32] if i == 0 else (), j=() if i == 0 else [kb32])
                    pvb = proj(2704, 768)
                    for i, (bk, n, c) in enumerate(pvb):
                        E("dve", lambda e, bk=bk, n=n, c=c, VS=VS: e.tensor_copy(out=VS[:, c:c + n], in_=bk[:, 0:n]),
                          r=[bk], w=[VS] if i == 0 else (), j=() if i == 0 else [VS])
                if KSUB < 3:
                    continue
                go = 384 if smp else 0
                ba = bank()
                bb = bank()
                mmgroup(ba[:, 0:512], ba, [(gmb[:, go + 128:go + 256], spbb[:, 0:512])], [gmb, spbb])
                mmgroup(bb[:, 0:128], bb, [(gmb[:, go + 128:go + 256], spbb[:, 512:640])], [gmb, spbb])
                E("act", lambda e, ba=ba: e.activation(out=kdec[:, 0:512], in_=ba[:, 0:512], func=AF.Exp), r=[ba], w=[kdec])
                E("act", lambda e, bb=bb: e.activation(out=kdec[:, 512:640], in_=bb[:, 0:128], func=AF.Exp), r=[bb], j=[kdec])
                E("dve", lambda e: e.tensor_tensor(out=khat[:], in0=ksb[:], in1=kdec[:], op=ALU.mult),
                  r=[ksb, kdec], w=[khat])
                if KSUB < 4:
                    continue
                nch = 4 if smp else 2
                io = 772 if smp else 768
                bk = bank()
                for m in range(5):
                    mmgroup(bk[:, m * nch:(m + 1) * nch], bk, [(spbb[:, m * 128:(m + 1) * 128], gmb[:, io:io + nch])],
                            [spbb, gmb], fresh=(m == 0))
                E("act", lambda e, bk=bk, nch=nch: e.activation(out=Dsb[:, 0:5 * nch], in_=bk[:, 0:5 * nch], func=AF.Exp),
                  r=[bk], w=[Dsb])
                if mixer and KSEC >= 3:
                    ba = bank()
                    bb = bank()
                    mmgroup(ba[:, 0:512], ba, [(gmb[:, go:go + 128], spbb[:, 0:512])], [gmb, spbb])
                    mmgroup(bb[:, 0:128], bb, [(gmb[:, go:go + 128], spbb[:, 512:640])], [gmb, spbb])
                    E("act", lambda e, ba=ba: e.activation(out=eqb[:, 0:512], in_=ba[:, 0:512], func=AF.Exp), r=[ba], w=[eqb])
                    E("act", lambda e, bb=bb: e.activation(out=eqb[:, 512:640], in_=bb[:, 0:128], func=AF.Exp), r=[bb], j=[eqb])
                    E("act", lambda e, ba=ba: e.activation(out=ekb[:, 0:512], in_=ba[:, 0:512], func=AF.Exp, scale=-1.0), r=[ba], w=[ekb])
                    E("act", lambda e, bb=bb: e.activation(out=ekb[:, 512:640], in_=bb[:, 0:128], func=AF.Exp, scale=-1.0), r=[bb], j=[ekb])
                    D("sp", lambda e, mi=mi: e.dma_start(out=g_eq[mi].ap, in_=eqb[:]), r=[eqb], w=[g_eq[mi]])
                    E("dve", lambda e: e.tensor_tensor(out=ktl[:], in0=ksb[:], in1=ekb[:], op=ALU.mult), r=[ksb, ekb], w=[ktl])
                    bkf = bankbf()
                    transposes([(ktl[:, m * 128:(m + 1) * 128], 128) for m in range(5)], bkf, 128, [ktl], ident)
                    E("act", lambda e, bkf=bkf: e.copy(out=kTt[:], in_=bkf[:, 0:640]), r=[bkf], w=[kTt])
                    D("sp", lambda e, mi=mi: e.dma_start(out=g_kT[mi].ap, in_=kTt[:]), r=[kTt], w=[g_kT[mi]])
                if KSEC < 5:
                    continue
                if not smp:
                    for ch in range(2):
                        if mixer:
                            E("act", lambda e, ch=ch: e.copy(out=Sbf[:, ch * 640:(ch + 1) * 640], in_=Sst[:]),
                              r=[Sst], w=[Sbf] if ch == 0 else (), j=() if ch == 0 else [Sbf])
                        ba = bank()
                        bb = bank()
                        for m in range(5):
                            tgt = ba if m < 4 else bb
                            mc = (m % 4) * 128
                            for h2 in range(2):
                                h = 2 * m + h2
                                mmgroup(tgt[64 * h2:64 * h2 + 64, mc:mc + 128], tgt,
                                        [(khat[64 * ch:64 * ch + 64, h * 64:(h + 1) * 64],
                                          VB[64 * ch:64 * ch + 64, h * 128:(h + 1) * 128])],
                                        [khat, VB], fresh=(m % 4 == 0 and h2 == 0))
                        for m in range(5):
                            tgt = ba if m < 4 else bb
                            mc = (m % 4) * 128
                            E("dve", lambda e, m=m, tgt=tgt, mc=mc, ch=ch: e.scalar_tensor_tensor(
                                out=Sst[:, m * 128:(m + 1) * 128], in0=Sst[:, m * 128:(m + 1) * 128],
                                scalar=Dsb[:, 2 * m + ch:2 * m + ch + 1], in1=tgt[:, mc:mc + 128],
                                op0=ALU.mult, op1=ALU.add), r=[Sst, Dsb, tgt], w=[Sst])
                    if mixer:
                        D("sp", lambda e, mi=mi: e.dma_start(out=g_S[mi].ap[:, 0:1280], in_=Sbf[:, 0:1280]),
                          r=[Sbf], w=[g_S[mi]])
                    if t == 31:
                        D("sp", lambda e: e.dma_start(out=gla_end.ap, in_=Sst[:]), r=[Sst], w=[gla_end])
                else:
                    for sq in range(4):
                        S0 = S0q[sq % 2]
                        S1 = S1q[sq % 2]
                        D("sp", lambda e, sq=sq, S0=S0: e.dma_start(
                            out=S0[:, :].rearrange("p (m v) -> p m v", m=5),
                            in_=sgla_d[sq].rearrange("(m p) v -> p m v", p=128)), w=[S0])
                        E("act", lambda e, sq=sq, S0=S0: e.copy(out=Sbf[:, sq * 640:(sq + 1) * 640], in_=S0[:]),
                          r=[S0], w=[Sbf] if sq == 0 else (), j=() if sq == 0 else [Sbf])
                        E("dve", lambda e, sq=sq: e.tensor_scalar(
                            out=ktl[:], in0=khat[:], scalar1=smask[:, 64 + sq:65 + sq], scalar2=None, op0=ALU.mult),
                          r=[khat, smask], w=[ktl])
                        ba = bank()
                        bb = bank()
                        for m in range(5):
                            tgt = ba if m < 4 else bb
                            mc = (m % 4) * 128
                            for h2 in range(2):
                                h = 2 * m + h2
                                mmgroup(tgt[64 * h2:64 * h2 + 64, mc:mc + 128], tgt,
                                        [(ktl[0:16, h * 64:(h + 1) * 64], VB[0:16, h * 128:(h + 1) * 128])],
                                        [ktl, VB], fresh=(m % 4 == 0 and h2 == 0))
                        for m in range(5):
                            tgt = ba if m < 4 else bb
                            mc = (m % 4) * 128
                            E("dve", lambda e, m=m, tgt=tgt, mc=mc, sq=sq, S0=S0, S1=S1: e.scalar_tensor_tensor(
                                out=S1[:, m * 128:(m + 1) * 128], in0=S0[:, m * 128:(m + 1) * 128],
                                scalar=Dsb[:, 4 * m + sq:4 * m + sq + 1], in1=tgt[:, mc:mc + 128],
                                op0=ALU.mult, op1=ALU.add), r=[S0, Dsb, tgt], w=[S1] if m == 0 else (), j=() if m == 0 else [S1])
                        D("sp", lambda e, sq=sq, S1=S1: e.dma_start(out=gla_smp.ap[sq], in_=S1[:]), r=[S1], j=[gla_smp])
                    D("sp", lambda e, mi=mi: e.dma_start(out=g_S[mi].ap, in_=Sbf[:]), r=[Sbf], w=[g_S[mi]])
                if halo and KSEC >= 6:
                    CS = cst_[b2]
                    KR = krot[b2]
                    k4 = kb32[:, :].rearrange("p (h two d) -> p h two d", h=12, two=2)
                    o4 = KR[:, :].rearrange("p (h two d) -> p h two d", h=12, two=2)
                    cosb = CS[:, 0:32].unsqueeze(1).to_broadcast([128, 12, 32])
                    sinb = CS[:, 32:64].unsqueeze(1).to_broadcast([128, 12, 32])
                    r4 = [x[:, :].rearrange("p (h d) -> p h d", h=12) for x in rt]
                    E("pool", lambda e, r4=r4, k4=k4, cosb=cosb: e.tensor_tensor(out=r4[0], in0=k4[:, :, 0, :], in1=cosb, op=ALU.mult), r=[kb32, CS], w=[rt[0]])
                    E("pool", lambda e, r4=r4, k4=k4, sinb=sinb: e.tensor_tensor(out=r4[1], in0=k4[:, :, 1, :], in1=sinb, op=ALU.mult), r=[kb32, CS], w=[rt[1]])
                    E("pool", lambda e, r4=r4, k4=k4, cosb=cosb: e.tensor_tensor(out=r4[2], in0=k4[:, :, 1, :], in1=cosb, op=ALU.mult), r=[kb32, CS], w=[rt[2]])
                    E("pool", lambda e, r4=r4, k4=k4, sinb=sinb: e.tensor_tensor(out=r4[3], in0=k4[:, :, 0, :], in1=sinb, op=ALU.mult), r=[kb32, CS], w=[rt[3]])
                    E("dve", lambda e, r4=r4, o4=o4: e.tensor_tensor(out=o4[:, :, 0, :], in0=r4[0], in1=r4[1], op=ALU.subtract), r=[rt[0], rt[1]], w=[KR])
                    E("dve", lambda e, r4=r4, o4=o4: e.tensor_tensor(out=o4[:, :, 1, :], in0=r4[2], in1=r4[3], op=ALU.add), r=[rt[2], rt[3]], j=[KR])
                    if 24 <= t < 32:
                        D("sp", lambda e, KR=KR, t=t: e.dma_start(out=k_main.ap[(t - 24) * 128:(t - 23) * 128, :], in_=KR[:]),
                          r=[KR], j=[k_main])
                    if smp:
                        for sq in range(4):
                            D("sp", lambda e, KR=KR, sq=sq: e.dma_start(out=ks_out.ap[sq, 2044:2048, :], in_=KR[4 * sq:4 * sq + 4, :]),
                              r=[KR], j=[ks_out])
                    E("act", lambda e, KR=KR: e.copy(out=kbf[:], in_=KR[:]), r=[KR], w=[kbf])
                    bkf = bankbf()
                    transposes([(kbf[:, m * 128:(m + 1) * 128], 128) for m in range(6)], bkf, 128, [kbf], ident)
                    KT_ = KTt[b2]
                    E("act", lambda e, bkf=bkf, KT_=KT_: e.copy(out=KT_[:], in_=bkf[:, 0:768]), r=[bkf], w=[KT_])
                    D("sp", lambda e, KT_=KT_, kb=kb: e.dma_start(out=KTs[kb].ap, in_=KT_[:]), r=[KT_], w=[KTs[kb]])
                    VS = vsb[b2]
                    VA = vaug[b2]
                    E("pool", lambda e, VS=VS, VA=VA: e.tensor_copy(
                        out=VA[:, :, 0:64], in_=VS[:, :].rearrange("p (h d) -> p h d", h=12)), r=[VS], w=[VA])
                    if 24 <= t < 32:
                        D("sp", lambda e, VS=VS, t=t: e.dma_start(out=v_main.ap[(t - 24) * 128:(t - 23) * 128, :], in_=VS[:]),
                          r=[VS], j=[v_main])
                    if smp:
                        for sq in range(4):
                            D("sp", lambda e, VS=VS, sq=sq: e.dma_start(out=vs_out.ap[sq, 2044:2048, :], in_=VS[4 * sq:4 * sq + 4, :]),
                              r=[VS], j=[vs_out])
                    D("sp", lambda e, VA=VA, kb=kb: e.dma_start(out=Vs[kb].ap, in_=VA[:].rearrange("p h d -> p (h d)")),
                      r=[VA], w=[Vs[kb]])
            S.barrier()
            S.replay()
        if stop_after == "A":
            return nc

        def load(eng, dst, src_dt, src_ap=None):
            D(eng, lambda e: e.dma_start(out=dst[:] if not isinstance(dst, tuple) else dst[1],
                                         in_=src_dt.ap if src_ap is None else src_ap),
              r=[src_dt], w=[dst if not isinstance(dst, tuple) else dst[0]])

        with ExitStack() as ph:
            WB = T(nc, ph, "WB", [128, 16, 2688], BF16)
            for (c0, w_, off) in [(0, 640, 0), (2560, 1280, 640), (3856, 768, 1920)]:
                for kq in range(4):
                    D("pool", lambda e, c0=c0, w_=w_, off=off, kq=kq: e.dma_start(
                        out=WB[:, 4 * kq:4 * kq + 4, off:off + w_], in_=win_v[:, 4 * kq:4 * kq + 4, c0:c0 + w_]),
                      j=[WB])
            gnb = T(nc, ph, "gnb", [128, 128], F32)
            D("sp", lambda e: e.dma_start(out=gnb[:], in_=gn_d.to_broadcast([128, 128])), w=[gnb])
            XTb = [T(nc, ph, "XTb%d" % i, [128, 16, 128], BF16) for i in range(2)]
            eqt = [T(nc, ph, "eqt%d" % i, [128, 640], F32) for i in range(2)]
            csb = [T(nc, ph, "csb%d" % i, [128, 64], F32) for i in range(2)]
            qtl = T(nc, ph, "qtl", [128, 640], BF16)
            qTt = [T(nc, ph, "qTt%d" % i, [128, 640], BF16) for i in range(2)]
            rsl = T(nc, ph, "rsl", [128, 1280], F32)
            gat = [T(nc, ph, "gat%d" % i, [128, 1280], BF16) for i in range(2)]
            qb32 = T(nc, ph, "qb32", [128, 768], F32)
            qrot = T(nc, ph, "qrot", [128, 768], F32)
            rtb = [T(nc, ph, "rtb%d" % i, [128, 384], F32) for i in range(4)]
            qbf = T(nc, ph, "qbf", [128, 768], BF16)
            QTt = [T(nc, ph, "QTt%d" % i, [128, 768], BF16) for i in range(2)]
            for mi in range(10):
                t = 23 + mi if mi < 9 else T_SMP
                b2 = mi % 2
                XT = XTb[b2]
                D("sp", lambda e, XT=XT, mi=mi: e.dma_start(
                    out=XT[:], in_=xTs[mi].ap.rearrange("p (a b) -> p a b", a=16)), r=[xTs[mi]], w=[XT])
                EQ = eqt[b2]
                D("sp", lambda e, EQ=EQ, mi=mi: e.dma_start(out=EQ[:], in_=g_eq[mi].ap), r=[g_eq[mi]], w=[EQ])
                CS = csb[b2]
                D("sp", lambda e, CS=CS, t=t: e.dma_start(out=CS[:], in_=cs_d[t * 128:(t + 1) * 128, :]), w=[CS])

                def projB(off, width, XT=XT):
                    res = []
                    c = 0
                    while c < width:
                        n = min(512, width - c)
                        bk = bank()
                        mmgroup(bk[:, 0:n], bk,
                                [(XT[:, kc, :], WB[:, kc, off + c:off + c + n]) for kc in range(16)], [XT, WB])
                        res.append((bk, n, c))
                        c += n
                    return res
                for i, (bk, n, c) in enumerate(projB(0, 640)):
                    E("dve", lambda e, bk=bk, n=n, c=c, EQ=EQ: e.scalar_tensor_tensor(
                        out=qtl[:, c:c + n], in0=bk[:, 0:n], scalar=0.125, in1=EQ[:, c:c + n],
                        op0=ALU.mult, op1=ALU.mult), r=[bk, EQ], w=[qtl] if i == 0 else (), j=() if i == 0 else [qtl])
                bkf = bankbf()
                transposes([(qtl[:, m * 128:(m + 1) * 128], 128) for m in range(5)], bkf, 128, [qtl], ident)
                QA = qTt[b2]
                E("act", lambda e, bkf=bkf, QA=QA: e.copy(out=QA[:], in_=bkf[:, 0:640]), r=[bkf], w=[QA])
                D("sp", lambda e, QA=QA, mi=mi: e.dma_start(out=g_qT[mi].ap, in_=QA[:]), r=[QA], w=[g_qT[mi]])
                for i, (bk, n, c) in enumerate(projB(640, 1280)):
                    E("act", lambda e, bk=bk, n=n, c=c: e.activation(out=rsl[:, c:c + n], in_=bk[:, 0:n], func=AF.Silu),
                      r=[bk], w=[rsl] if i == 0 else (), j=() if i == 0 else [rsl])
                GA = gat[b2]
                E("pool", lambda e, GA=GA: e.tensor_tensor(
                    out=GA[:, :].rearrange("p (h d) -> p h d", h=10), in0=rsl[:, :].rearrange("p (h d) -> p h d", h=10),
                    in1=gnb[:, :].unsqueeze(1).to_broadcast([128, 10, 128]), op=ALU.mult), r=[rsl, gnb], w=[GA])
                D("sp", lambda e, GA=GA, mi=mi: e.dma_start(out=g_gate[mi].ap, in_=GA[:]), r=[GA], w=[g_gate[mi]])
                for i, (bk, n, c) in enumerate(projB(1920, 768)):
                    E("act", lambda e, bk=bk, n=n, c=c: e.copy(out=qb32[:, c:c + n], in_=bk[:, 0:n]),
                      r=[bk], w=[qb32] if i == 0 else (), j=() if i == 0 else [qb32])
                k4 = qb32[:, :].rearrange("p (h two d) -> p h two d", h=12, two=2)
                o4 = qrot[:, :].rearrange("p (h two d) -> p h two d", h=12, two=2)
                cosb = CS[:, 0:32].unsqueeze(1).to_broadcast([128, 12, 32])
                sinb = CS[:, 32:64].unsqueeze(1).to_broadcast([128, 12, 32])
                r4 = [x[:, :].rearrange("p (h d) -> p h d", h=12) for x in rtb]
                E("pool", lambda e, r4=r4, k4=k4, cosb=cosb: e.tensor_tensor(out=r4[0], in0=k4[:, :, 0, :], in1=cosb, op=ALU.mult), r=[qb32, CS], w=[rtb[0]])
                E("pool", lambda e, r4=r4, k4=k4, sinb=sinb: e.tensor_tensor(out=r4[1], in0=k4[:, :, 1, :], in1=sinb, op=ALU.mult), r=[qb32, CS], w=[rtb[1]])
                E("pool", lambda e, r4=r4, k4=k4, cosb=cosb: e.tensor_tensor(out=r4[2], in0=k4[:, :, 1, :], in1=cosb, op=ALU.mult), r=[qb32, CS], w=[rtb[2]])
                E("pool", lambda e, r4=r4, k4=k4, sinb=sinb: e.tensor_tensor(out=r4[3], in0=k4[:, :, 0, :], in1=sinb, op=ALU.mult), r=[qb32, CS], w=[rtb[3]])
                E("dve", lambda e, r4=r4, o4=o4: e.tensor_tensor(out=o4[:, :, 0, :], in0=r4[0], in1=r4[1], op=ALU.subtract), r=[rtb[0], rtb[1]], w=[qrot])
                E("dve", lambda e, r4=r4, o4=o4: e.tensor_tensor(out=o4[:, :, 1, :], in0=r4[2], in1=r4[3], op=ALU.add), r=[rtb[2], rtb[3]], j=[qrot])
                E("act", lambda e: e.copy(out=qbf[:], in_=qrot[:]), r=[qrot], w=[qbf])
                bkf = bankbf()
                transposes([(qbf[:, m * 128:(m + 1) * 128], 128) for m in range(6)], bkf, 128, [qbf], ident)
                QB = QTt[b2]
                E("act", lambda e, bkf=bkf, QB=QB: e.copy(out=QB[:], in_=bkf[:, 0:768]), r=[bkf], w=[QB])
                D("sp", lambda e, QB=QB, mi=mi: e.dma_start(out=g_QT[mi].ap, in_=QB[:]), r=[QB], w=[g_QT[mi]])
            S.barrier()
            S.replay()
        if stop_after == "B":
            return nc

        def gla_out(qT, kT, vv, Ssn, gate, ofull, PT, junkf, ssh, tmpf, maskG_ap, state_terms, osum):
            of4 = ofull[:, 0:1280].rearrange("p (a two b) -> p a two b", two=2, b=128)
            ga4 = gate[:, 0:1280].rearrange("p (a two b) -> p a two b", two=2, b=128)
            firstw = [True]
            for (a0, a1) in ((0, 4), (4, 5)):
                for h2 in range(2):
                    n = a1 - a0
                    heads = [2 * a + h2 for a in range(a0, a1)]
                    Sps = bank()
                    for hl, h in enumerate(heads):
                        m = h // 2
                        for jh in range(2):
                            mmgroup(Sps[64 * jh:64 * jh + 64, hl * 128:(hl + 1) * 128], Sps,
                                    [(kT[64 * h2:64 * h2 + 64, m * 128 + 64 * jh:m * 128 + 64 * jh + 64],
                                      qT[64 * h2:64 * h2 + 64, m * 128:(m + 1) * 128])],
                                    [kT, qT], fresh=(hl == 0 and jh == 0))
                    E("dve", lambda e, Sps=Sps, n=n: e.tensor_tensor(
                        out=PT[:, 0:n * 128].rearrange("p (a b) -> p a b", a=n),
                        in0=Sps[:, 0:n * 128].rearrange("p (a b) -> p a b", a=n),
                        in1=maskG_ap.unsqueeze(1).to_broadcast([128, n, 128]), op=ALU.mult), r=[Sps, gm], w=[PT])
                    Ops = bank()
                    Ops2 = bank()
                    for hl, h in enumerate(heads):
                        mmgroup(Ops[:, hl * 128:(hl + 1) * 128], Ops, [(PT[:, hl * 128:(hl + 1) * 128], vv[:, h * 128:(h + 1) * 128])],
                                [PT, vv], fresh=(hl == 0))
                    first2 = True
                    for hl, h in enumerate(heads):
                        m = h // 2
                        for (r0, r1, pairs, rds) in state_terms(h, m, h2):
                            mmgroup(Ops2[r0:r1, hl * 128:(hl + 1) * 128], Ops2, pairs, rds, fresh=first2)
                            first2 = False
                    E("act", lambda e, Ops2=Ops2, n=n: e.copy(out=tmpf[:, 0:n * 128], in_=Ops2[:, 0:n * 128]), r=[Ops2], w=[tmpf])
                    E("dve", lambda e, Ops=Ops, n=n: e.tensor_tensor(out=osum[:, 0:n * 128], in0=Ops[:, 0:n * 128], in1=tmpf[:, 0:n * 128], op=ALU.add),
                      r=[Ops, tmpf], w=[osum])
                    E("act", lambda e, n=n: e.activation(out=junkf[:, 0:n * 128], in_=osum[:, 0:n * 128], func=AF.Square),
                      r=[osum], w=[junkf])
                    E("dve", lambda e, n=n: e.tensor_reduce(out=ssh[:, 0:n], in_=junkf[:, 0:n * 128].rearrange("p (a b) -> p a b", a=n),
                                                            axis=AX.X, op=ALU.add), r=[junkf], w=[ssh])
                    E("act", lambda e, n=n: e.activation(out=ssh[:, 4:4 + n], in_=ssh[:, 0:n], func=AF.Sqrt, scale=1.0 / 128, bias=EPS),
                      r=[ssh], w=[ssh])
                    E("dve", lambda e, n=n: e.reciprocal(out=ssh[:, 8:8 + n], in_=ssh[:, 4:4 + n]), r=[ssh], w=[ssh])
                    E("dve", lambda e, n=n: e.tensor_tensor(
                        out=tmpf[:, 0:n * 128].rearrange("p (a b) -> p a b", a=n),
                        in0=osum[:, 0:n * 128].rearrange("p (a b) -> p a b", a=n),
                        in1=ssh[:, 8:8 + n].unsqueeze(2).to_broadcast([128, n, 128]), op=ALU.mult), r=[osum, ssh], w=[tmpf])
                    fw = firstw[0]
                    firstw[0] = False
                    E("dve", lambda e, n=n, a0=a0, a1=a1, h2=h2: e.tensor_tensor(
                        out=of4[:, a0:a1, h2, :], in0=tmpf[:, 0:n * 128].rearrange("p (a b) -> p a b", a=n),
                        in1=ga4[:, a0:a1, h2, :], op=ALU.mult),
                      r=[tmpf, gate], w=[ofull] if fw else (), j=() if fw else [ofull])

        def otrans_store(ofull, oTt, mi):
            for q4 in range(2):
                bk = bankbf()
                transposes([(ofull[:, (q4 * 8 + i) * 128:(q4 * 8 + i + 1) * 128], 128) for i in range(8)], bk, 128, [ofull], ident)
                if q4 == 0:
                    E("act", lambda e, bk=bk: e.copy(out=oTt[:, 0:8, :], in_=bk[:, :].rearrange("p (a b) -> p a b", a=8)),
                      r=[bk], w=[oTt])
                else:
                    E("dve", lambda e, bk=bk: e.tensor_copy(out=oTt[:, 8:16, :], in_=bk[:, :].rearrange("p (a b) -> p a b", a=8)),
                      r=[bk], j=[oTt])
            D("sp", lambda e: e.dma_start(out=g_oT[mi].ap.rearrange("p (a b) -> p a b", a=16), in_=oTt[:]), r=[oTt], w=[g_oT[mi]])

        with ExitStack() as ph:
            gmb2 = T(nc, ph, "gmb2", [128, 256], BF16)
            E("dve", lambda e: e.tensor_copy(out=gmb2[:, 0:128], in_=gm[:, 256:384]), r=[gm], w=[gmb2])
            E("dve", lambda e: e.tensor_copy(out=gmb2[:, 128:256], in_=gm[:, 640:768]), r=[gm], j=[gmb2])
            KTa = T(nc, ph, "KTa", [128, 6, NKB * 128], BF16)
            Va = T(nc, ph, "Va", [128, NKB, 780], BF16)
            for kb in range(NKB):
                D("sp", lambda e, kb=kb: e.dma_start(out=KTa[:, :, kb * 128:(kb + 1) * 128],
                                                     in_=KTs[kb].ap.rearrange("p (a b) -> p a b", a=6)), r=[KTs[kb]], j=[KTa])
                D("sp", lambda e, kb=kb: e.dma_start(out=Va[:, kb, :], in_=Vs[kb].ap), r=[Vs[kb]], j=[Va])
            for kb in range(25):
                E("pool", lambda e, kb=kb: e.tensor_scalar(out=Va[:, kb, :], in0=Va[:, kb, :], scalar1=flags[:, kb:kb + 1],
                                                           scalar2=None, op0=ALU.mult), r=[Va, flags], w=[Va])
            maskp = T(nc, ph, "maskp", [128, 17 * 128], BF16)
            D("pool", lambda e: e.dma_start(out=maskp[:], in_=maskp_d), w=[maskp])
            qTc = T(nc, ph, "qTc", [128, 640], BF16)
            kTc = T(nc, ph, "kTc", [128, 640], BF16)
            vc_ = T(nc, ph, "vc", [128, 1280], BF16)
            Ssc = T(nc, ph, "Ssc", [128, 1280], BF16)
            gtc = T(nc, ph, "gtc", [128, 1280], BF16)
            QTc = T(nc, ph, "QTc", [128, 768], BF16)
            ofull = T(nc, ph, "ofull", [128, 2048], BF16)
            PT = T(nc, ph, "PT", [128, 512], BF16)
            junkf = T(nc, ph, "junkf", [128, 512], F32)
            ssh = T(nc, ph, "ssh", [128, 12], F32)
            tmpf = T(nc, ph, "tmpf", [128, 512], F32)
            osum = T(nc, ph, "osum", [128, 512], F32)
            Pts = [T(nc, ph, "Pts%d" % i, [128, 512], BF16) for i in range(3)]
            rden = T(nc, ph, "rden", [128, 12], F32)
            oTt = T(nc, ph, "oTt", [128, 16, 128], BF16)
            pti = [0]
            import os as _os2
            KC = int(_os2.environ.get("KC", "9"))
            KCT = int(_os2.environ.get("KCT", "9"))
            for mi in range(KCT):
                if KC < 2:
                    continue
                for dst, src in ((qTc, g_qT[mi]), (kTc, g_kT[mi]), (vc_, g_v[mi]), (gtc, g_gate[mi]), (QTc, g_QT[mi])):
                    D("sp", lambda e, dst=dst, src=src: e.dma_start(out=dst[:], in_=src.ap), r=[src], w=[dst])
                D("sp", lambda e, mi=mi: e.dma_start(out=Ssc[:], in_=g_S[mi].ap[:, 0:1280]), r=[g_S[mi]], w=[Ssc])

                def st_terms(h, m, h2):
                    return [(0, 64, [(qTc[64 * h2:64 * h2 + 64, m * 128:m * 128 + 64],
                                      Ssc[64 * h2:64 * h2 + 64, m * 128:(m + 1) * 128])], [qTc, Ssc]),
                            (64, 128, [(qTc[64 * h2:64 * h2 + 64, m * 128 + 64:m * 128 + 128],
                                        Ssc[64 * h2:64 * h2 + 64, 640 + m * 128:640 + (m + 1) * 128])], [qTc, Ssc])]
                gla_out(qTc, kTc, vc_, Ssc, gtc, ofull, PT, junkf, ssh, tmpf, gm[:, 256:384], st_terms, osum)
                if KC < 3:
                    continue
                kb0 = mi
                Ob = [pb[4], pb[5]]
                reserved[:] = Ob
                groups = [(h, g) for h in range(12) for g in range(5)]

                def emit_scores(h, g):
                    hp, h2 = h // 2, h % 2
                    i0 = 4 * g
                    n = min(4, 17 - i0)
                    Sps = bank()
                    for ii in range(n):
                        kb = kb0 + i0 + ii
                        for jh in range(2):
                            mmgroup(Sps[64 * jh:64 * jh + 64, ii * 128:(ii + 1) * 128], Sps,
                                    [(KTa[64 * h2:64 * h2 + 64, hp, kb * 128 + 64 * jh:kb * 128 + 64 * jh + 64],
                                      QTc[64 * h2:64 * h2 + 64, hp * 128:(hp + 1) * 128])], [KTa, QTc],
                                    fresh=(ii == 0 and jh == 0))
                    return Sps

                def emit_rest(h, g, Sps):
                    ob = Ob[0] if h < 7 else Ob[1]
                    hr = h if h < 7 else h - 7
                    i0 = 4 * g
                    n = min(4, 17 - i0)
                    Pt = Pts[pti[0] % 3]
                    pti[0] += 1
                    E("act", lambda e, Sps=Sps, Pt=Pt, n=n: e.activation(out=Pt[:, 0:n * 128], in_=Sps[:, 0:n * 128],
                                                                       func=AF.Exp, scale=0.125), r=[Sps], w=[Pt])
                    E("dve", lambda e, Pt=Pt, n=n, i0=i0: e.tensor_tensor(out=Pt[:, 0:n * 128], in0=Pt[:, 0:n * 128],
                                                                         in1=maskp[:, i0 * 128:(i0 + n) * 128], op=ALU.mult),
                      r=[Pt, maskp], w=[Pt])
                    for ii in range(n):
                        i = i0 + ii
                        kb = kb0 + i
                        fn = (lambda e, ob=ob, hr=hr, Pt=Pt, ii=ii, kb=kb, h=h, i=i: e.matmul(
                            ob[:, hr * 65:hr * 65 + 65], lhsT=Pt[:, ii * 128:(ii + 1) * 128],
                            rhs=Va[:, kb, h * 65:(h + 1) * 65], start=(i == 0), stop=(i == 16)))
                        if hr == 0 and i == 0:
                            E("pe", fn, r=[Pt, Va], w=[ob], signal=(ii == n - 1))
                        else:
                            E("pe", fn, r=[Pt, Va], j=[ob], signal=(ii == n - 1))
                pend = emit_scores(*groups[0])
                for gi, (h, g) in enumerate(groups):
                    nxt = emit_scores(*groups[gi + 1]) if gi + 1 < len(groups) else None
                    emit_rest(h, g, pend)
                    pend = nxt
                for (ob, hA, nh) in ((Ob[0], 0, 7), (Ob[1], 7, 5)):
                    ov = ob[:, 0:nh * 65].rearrange("p (a b) -> p a b", a=nh)
                    E("dve", lambda e, ov=ov, hA=hA, nh=nh: e.tensor_scalar(
                        out=rden[:, hA:hA + nh].unsqueeze(2), in0=ov[:, :, 64:65], scalar1=1e-30, scalar2=None, op0=ALU.add),
                      r=[ob], w=[rden] if hA == 0 else (), j=() if hA == 0 else [rden])
                    E("dve", lambda e, hA=hA, nh=nh: e.reciprocal(out=rden[:, hA:hA + nh], in_=rden[:, hA:hA + nh]), r=[rden], w=[rden])
                    E("dve", lambda e, ov=ov, hA=hA, nh=nh: e.tensor_tensor(
                        out=ofull[:, 1280 + hA * 64:1280 + (hA + nh) * 64].rearrange("p (a b) -> p a b", a=nh),
                        in0=ov[:, :, 0:64], in1=rden[:, hA:hA + nh].unsqueeze(2).to_broadcast([128, nh, 64]), op=ALU.mult),
                      r=[ob, rden], j=[ofull])
                reserved[:] = []
                if KC < 4:
                    continue
                otrans_store(ofull, oTt, mi)
            S.barrier()
            S.replay()
        if stop_after == "C":
            return nc

        with ExitStack() as ph:
            gmb2 = T(nc, ph, "gmb2b", [128, 256], BF16)
            E("dve", lambda e: e.tensor_copy(out=gmb2[:, 0:128], in_=gm[:, 256:384]), r=[gm], w=[gmb2])
            E("dve", lambda e: e.tensor_copy(out=gmb2[:, 128:256], in_=gm[:, 640:768]), r=[gm], j=[gmb2])
            masks = T(nc, ph, "masks", [128, 1040], BF16)
            D("pool", lambda e: e.dma_start(out=masks[:], in_=masks_d), w=[masks])
            smb = T(nc, ph, "smb", [128, 64], BF16)
            E("dve", lambda e: e.tensor_copy(out=smb[:], in_=smask[:, 0:64]), r=[smask], w=[smb])
            qTc = T(nc, ph, "qTs", [128, 640], BF16)
            kTc = T(nc, ph, "kTs", [128, 640], BF16)
            vc_ = T(nc, ph, "vs", [128, 1280], BF16)
            Ssc = T(nc, ph, "Sss", [128, 2560], BF16)
            gtc = T(nc, ph, "gts", [128, 1280], BF16)
            QTc = T(nc, ph, "QTs", [128, 768], BF16)
            ofull = T(nc, ph, "ofulls", [128, 2048], BF16)
            PT = T(nc, ph, "PTs", [128, 512], BF16)
            junkf = T(nc, ph, "junkfs", [128, 512], F32)
            ssh = T(nc, ph, "sshs", [128, 12], F32)
            tmpf = T(nc, ph, "tmpfs", [128, 512], F32)
            osum = T(nc, ph, "osums", [128, 512], F32)
            oTt = T(nc, ph, "oTts", [128, 16, 128], BF16)
            qTm = [T(nc, ph, "qTm%d" % i, [128, 5, 128], BF16) for i in range(4)]
            KTn = T(nc, ph, "KTn", [128, 768], BF16)
            Vn = T(nc, ph, "Vn", [128, 780], BF16)
            Kc = T(nc, ph, "Kc", [128, 16, 768], BF16)
            KTc = T(nc, ph, "KTc", [128, 6, 2048], BF16)
            Vc = T(nc, ph, "Vc", [128, 16, 12, 65], BF16)
            Vst = T(nc, ph, "Vst", [128, 16, 768], BF16)
            Pts = [T(nc, ph, "Ptss%d" % i, [128, 64], BF16) for i in range(3)]
            Oacc = T(nc, ph, "Oacc", [16, 780], F32)
            rden = T(nc, ph, "rdens", [16, 12], F32)
            mi = 9
            for dst, src in ((qTc, g_qT[mi]), (kTc, g_kT[mi]), (vc_, g_v[mi]), (gtc, g_gate[mi]), (QTc, g_QT[mi]),
                             (Ssc, g_S[mi]), (KTn, KTs[25]), (Vn, Vs[25])):
                D("sp", lambda e, dst=dst, src=src: e.dma_start(out=dst[:], in_=src.ap), r=[src], w=[dst])
            for sq in range(4):
                E("pool", lambda e, sq=sq: e.memset(qTm[sq][:], 0.0), w=[qTm[sq]])
                E("dve", lambda e, sq=sq: e.tensor_tensor(
                    out=qTm[sq][:, :, 0:16], in0=qTc[:, :].rearrange("p (m t) -> p m t", m=5)[:, :, 0:16],
                    in1=smb[:, sq * 16:(sq + 1) * 16].unsqueeze(1).to_broadcast([128, 5, 16]), op=ALU.mult),
                  r=[qTc, smb], w=[qTm[sq]])

            def st_terms_s(h, m, h2):
                return [(64 * jh, 64 * jh + 64,
                         [(qTm[sq][64 * h2:64 * h2 + 64, m, 64 * jh:64 * jh + 64],
                           Ssc[64 * h2:64 * h2 + 64, sq * 640 + m * 128:sq * 640 + (m + 1) * 128]) for sq in range(4)],
                         qTm + [Ssc]) for jh in range(2)]
            E("pool", lambda e: e.memset(ofull[:, 1280:2048], 0.0), w=[ofull])
            gla_out(qTc, kTc, vc_, Ssc, gtc, ofull, PT, junkf, ssh, tmpf, gm[:, 640:768], st_terms_s, osum)
            E("pool", lambda e: e.memset(Vc[:], 1.0), w=[Vc])
            pti = [0]
            for sq in range(4):
                for b4 in range(4):
                    D("pool", lambda e, sq=sq, b4=b4: e.dma_start(
                        out=Kc[:, 4 * b4:4 * b4 + 4, :],
                        in_=ck_d[sq, 512 * b4:512 * b4 + 512, :].rearrange("(b p) c -> p b c", p=128)),
                      w=[Kc] if b4 == 0 else (), j=() if b4 == 0 else [Kc])
                for b4 in range(4):
                    D("pool", lambda e, sq=sq, b4=b4: e.dma_start(
                        out=Vst[:, 4 * b4:4 * b4 + 4, :],
                        in_=cv_d[sq, 512 * b4:512 * b4 + 512, :].rearrange("(b p) c -> p b c", p=128)),
                      w=[Vst] if b4 == 0 else (), j=() if b4 == 0 else [Vst])
                for b4 in range(4):
                    E("dve" if b4 % 2 else "pool", lambda e, b4=b4: e.tensor_copy(
                        out=Vc[:, 4 * b4:4 * b4 + 4, :, 0:64],
                        in_=Vst[:, 4 * b4:4 * b4 + 4, :].rearrange("p b (h d) -> p b h d", h=12)),
                      r=[Vst], w=[Vc] if b4 == 0 else (), j=() if b4 == 0 else [Vc])
                for hp in range(6):
                    for b8 in range(2):
                        bkf = bankbf()
                        transposes([(Kc[:, b8 * 8 + i, hp * 128:(hp + 1) * 128], 128) for i in range(8)], bkf, 128, [Kc], ident)
                        E("act" if b8 == 0 else "dve",
                          (lambda e, bkf=bkf, hp=hp, b8=b8: e.copy(out=KTc[:, hp, b8 * 1024:(b8 + 1) * 1024], in_=bkf[:, :]))
                          if b8 == 0 else
                          (lambda e, bkf=bkf, hp=hp, b8=b8: e.tensor_copy(out=KTc[:, hp, b8 * 1024:(b8 + 1) * 1024], in_=bkf[:, :])),
                          r=[bkf], w=[KTc] if (hp == 0 and b8 == 0) else (), j=() if (hp == 0 and b8 == 0) else [KTc])
                Ob = [pb[4], pb[5]]
                reserved[:] = Ob
                for h in range(12):
                    hp, h2 = h // 2, h % 2
                    ob = Ob[0] if h < 7 else Ob[1]
                    hr = h if h < 7 else h - 7
                    ngrp = 5 if sq == 0 else 4
                    for g in range(ngrp):
                        Sps = bank()
                        if g < 4:
                            for ii in range(4):
                                blk = 4 * g + ii
                                for jh in range(2):
                                    mmgroup(Sps[64 * jh:64 * jh + 64, ii * 16:(ii + 1) * 16], Sps,
                                            [(KTc[64 * h2:64 * h2 + 64, hp, blk * 128 + 64 * jh:blk * 128 + 64 * jh + 64],
                                              QTc[64 * h2:64 * h2 + 64, hp * 128:hp * 128 + 16])], [KTc, QTc],
                                            fresh=(ii == 0 and jh == 0))
                            ncol, moff = 64, (sq * 16 + 4 * g) * 16
                        else:
                            for jh in range(2):
                                mmgroup(Sps[64 * jh:64 * jh + 64, 0:16], Sps,
                                        [(KTn[64 * h2:64 * h2 + 64, hp * 128 + 64 * jh:hp * 128 + 64 * jh + 64],
                                          QTc[64 * h2:64 * h2 + 64, hp * 128:hp * 128 + 16])], [KTn, QTc], fresh=(jh == 0))
                            ncol, moff = 16, 1024
                        Pt = Pts[pti[0] % 3]
                        pti[0] += 1
                        E("act", lambda e, Sps=Sps, Pt=Pt, ncol=ncol: e.activation(out=Pt[:, 0:ncol], in_=Sps[:, 0:ncol],
                                                                                 func=AF.Exp, scale=0.125), r=[Sps], w=[Pt])
                        E("dve", lambda e, Pt=Pt, ncol=ncol, moff=moff: e.tensor_tensor(
                            out=Pt[:, 0:ncol], in0=Pt[:, 0:ncol], in1=masks[:, moff:moff + ncol], op=ALU.mult),
                          r=[Pt, masks], w=[Pt])
                        nblk = 4 if g < 4 else 1
                        for ii in range(nblk):
                            first = (g == 0 and ii == 0)
                            last = (g == ngrp - 1 and ii == nblk - 1)
                            if g < 4:
                                blk = 4 * g + ii
                                fn = (lambda e, ob=ob, hr=hr, Pt=Pt, ii=ii, blk=blk, h=h, first=first, last=last: e.matmul(
                                    ob[0:16, hr * 65:hr * 65 + 65], lhsT=Pt[:, ii * 16:(ii + 1) * 16],
                                    rhs=Vc[:, blk, h, :], start=first, stop=last))
                                rd = [Pt, Vc]
                            else:
                                fn = (lambda e, ob=ob, hr=hr, Pt=Pt, h=h, first=first, last=last: e.matmul(
                                    ob[0:16, hr * 65:hr * 65 + 65], lhsT=Pt[:, 0:16],
                                    rhs=Vn[:, h * 65:(h + 1) * 65], start=first, stop=last))
                                rd = [Pt, Vn]
                            if hr == 0 and first:
                                E("pe", fn, r=rd, w=[ob], signal=True)
                            else:
                                E("pe", fn, r=rd, j=[ob], signal=True)
                for (ob, hA, nh) in ((Ob[0], 0, 7), (Ob[1], 7, 5)):
                    if sq == 0:
                        E("dve", lambda e, ob=ob, hA=hA, nh=nh: e.tensor_copy(out=Oacc[:, hA * 65:(hA + nh) * 65], in_=ob[0:16, 0:nh * 65]),
                          r=[ob], w=[Oacc] if hA == 0 else (), j=() if hA == 0 else [Oacc])
                    else:
                        E("dve", lambda e, ob=ob, hA=hA, nh=nh: e.tensor_tensor(
                            out=Oacc[:, hA * 65:(hA + nh) * 65], in0=Oacc[:, hA * 65:(hA + nh) * 65], in1=ob[0:16, 0:nh * 65], op=ALU.add),
                          r=[ob, Oacc], w=[Oacc])
            reserved[:] = []
            ov = Oacc[:, :].rearrange("p (a b) -> p a b", a=12)
            E("dve", lambda e: e.tensor_scalar(out=rden[:, :].unsqueeze(2), in0=ov[:, :, 64:65], scalar1=1e-30, scalar2=None, op0=ALU.add),
              r=[Oacc], w=[rden])
            E("dve", lambda e: e.reciprocal(out=rden[:], in_=rden[:]), r=[rden], w=[rden])
            E("dve", lambda e: e.tensor_tensor(
                out=ofull[0:16, 1280:2048].rearrange("p (a b) -> p a b", a=12), in0=ov[:, :, 0:64],
                in1=rden[:, :].unsqueeze(2).to_broadcast([16, 12, 64]), op=ALU.mult), r=[Oacc, rden], j=[ofull])
            otrans_store(ofull, oTt, mi)
            S.barrier()
            S.replay()
        if stop_after == "C2":
            return nc

        with ExitStack() as phDG:
            hnT = T(nc, phDG, "hnT", [128, 16, 1042], BF16)
            with ExitStack() as ph:
                wout_v = wout_d.rearrange("(kc p) c -> p kc c", p=128)
                WO = T(nc, ph, "WO", [128, 16, 2048], BF16)
                for kq in range(4):
                    D("pool", lambda e, kq=kq: e.dma_start(out=WO[:, 4 * kq:4 * kq + 4, :], in_=wout_v[:, 4 * kq:4 * kq + 4, :]), j=[WO])
                gffn = T(nc, ph, "gffn", [128, 2048], F32)
                D("sp", lambda e: e.dma_start(out=gffn[:], in_=nffn_d.to_broadcast([128, 2048])), w=[gffn])
                oTd = [T(nc, ph, "oTd%d" % i, [128, 16, 128], BF16) for i in range(2)]
                Xd = [T(nc, ph, "Xd%d" % i, [128, 2048], F32) for i in range(2)]
                h1t = [T(nc, ph, "h1t%d" % i, [128, 2048], F32) for i in range(2)]
                hnb = T(nc, ph, "hnb", [128, 2048], BF16)
                ssd = [T(nc, ph, "ssd%d" % i, [128, 2], F32) for i in range(2)]
                rsd = [T(nc, ph, "rsd%d" % i, [128, 1], F32) for i in range(2)]
                for mi in range(10):
                    t = 23 + mi if mi < 9 else T_SMP
                    b2 = mi % 2
                    OT, X, H1 = oTd[b2], Xd[b2], h1t[b2]
                    D("sp", lambda e, OT=OT, mi=mi: e.dma_start(out=OT[:], in_=g_oT[mi].ap.rearrange("p (a b) -> p a b", a=16)),
                      r=[g_oT[mi]], w=[OT])
                    D("sp", lambda e, X=X, t=t: e.dma_start(out=X[:], in_=xall[t * 128:(t + 1) * 128, :]), w=[X])
                    for n in range(4):
                        bk = bank()
                        mmgroup(bk[:, 0:512], bk, [(OT[:, kc, :], WO[:, kc, n * 512:(n + 1) * 512]) for kc in range(16)], [OT, WO])
                        E("dve", lambda e, bk=bk, n=n, X=X, H1=H1: e.tensor_tensor(
                            out=H1[:, n * 512:(n + 1) * 512], in0=bk[:, 0:512], in1=X[:, n * 512:(n + 1) * 512], op=ALU.add),
                          r=[bk, X], w=[H1] if n == 0 else (), j=() if n == 0 else [H1])
                    D("sp", lambda e, H1=H1, mi=mi: e.dma_start(out=g_h1[mi].ap, in_=H1[:]), r=[H1], w=[g_h1[mi]])
                    rms_rstd(H1[:], hnb, ssd[b2], rsd[b2], H1, 2048)
                    E("dve", lambda e, H1=H1, b2=b2: e.scalar_tensor_tensor(
                        out=hnb[:], in0=H1[:], scalar=rsd[b2][:, 0:1], in1=gffn[:], op0=ALU.mult, op1=ALU.mult),
                      r=[H1, rsd[b2], gffn], w=[hnb])
                    if mi == 0:
                        E("dve", lambda e: e.tensor_scalar(out=hnb[:], in0=hnb[:], scalar1=flags[:, 26:27], scalar2=None, op0=ALU.mult),
                          r=[hnb, flags], w=[hnb])
                    if mi == 0:
                        src0, ncol, dst0 = 126, 2, 0
                    elif mi < 9:
                        src0, ncol, dst0 = 0, 128, 2 + 128 * (mi - 1)
                    else:
                        src0, ncol, dst0 = 0, 16, 1026
                    for q4 in range(2):
                        bk = bankbf()
                        transposes([(hnb[:, (q4 * 8 + i) * 128:(q4 * 8 + i + 1) * 128], 128) for i in range(8)], bk, 128, [hnb], ident)
                        fnc = (lambda e, bk=bk, q4=q4, src0=src0, ncol=ncol, dst0=dst0: (e.copy if q4 == 0 else e.tensor_copy)(
                            out=hnT[:, q4 * 8:q4 * 8 + 8, dst0:dst0 + ncol],
                            in_=bk[:, :].rearrange("p (a b) -> p a b", a=8)[:, :, src0:src0 + ncol]))
                        E("act" if q4 == 0 else "dve", fnc, r=[bk], j=[hnT])
                S.barrier()
                S.replay()
            if stop_after == "D":
                return nc

            actT = T(nc, phDG, "actT", [128, NF, 1040], BF16)
            with ExitStack() as ph:
                wup_v = wup_d.rearrange("(kc p) c -> p kc c", p=128)
                wu = [T(nc, ph, "wu%d" % i, [128, 16, 256], BF16) for i in range(3)]
                tu = [T(nc, ph, "tu%d" % i, [128, 512], F32) for i in range(2)]
                tg = [T(nc, ph, "tg%d" % i, [128, 512], F32) for i in range(2)]
                sgt = [T(nc, ph, "sgt%d" % i, [128, 512], F32) for i in range(2)]
                Eb = [T(nc, ph, "Eb%d" % i, [128, 4, 6], F32) for i in range(2)]
                tS = [T(nc, ph, "tS%d" % i, [128, 16], F32) for i in range(2)]
                sgS = T(nc, ph, "sgS", [128, 16], F32)
                upl = [T(nc, ph, "upl%d" % i, [128, 18], F32) for i in range(2)]
                ups = [T(nc, ph, "ups%d" % i, [18, 128], F32) for i in range(2)]
                chunks = [(0, 512, 0), (510, 512, 510), (1020, 22, 1020)]
                cnt = [0]
                for f in range(NF):
                    W = wu[f % 3]
                    for kq in range(4):
                        D("pool", lambda e, W=W, f=f, kq=kq: e.dma_start(
                            out=W[:, 4 * kq:4 * kq + 4, 0:128], in_=wup_v[:, 4 * kq:4 * kq + 4, f * 128:(f + 1) * 128]),
                          w=[W] if kq == 0 else (), j=() if kq == 0 else [W])
                        D("pool", lambda e, W=W, f=f, kq=kq: e.dma_start(
                            out=W[:, 4 * kq:4 * kq + 4, 128:256],
                            in_=wup_v[:, 4 * kq:4 * kq + 4, D_FF + f * 128:D_FF + (f + 1) * 128]), j=[W])
                    for (c0, n, a0) in chunks:
                        k = cnt[0] % 2
                        cnt[0] += 1
                        TU, TG, SG = tu[k], tg[k], sgt[k]
                        res = []
                        for half, tt in ((0, TU), (1, TG)):
                            fi = f + NF * half
                            bk = bank()
                            mmgroup(bk[:, 0:n], bk, [(W[:, kc, half * 128:(half + 1) * 128], hnT[:, kc, c0:c0 + n]) for kc in range(16)],
                                    [W, hnT])
                            npr = (n - 2) if n == 512 else 4
                            E("act", lambda e, bk=bk, tt=tt, fi=fi, npr=npr: e.activation(
                                out=tt[:, 0:npr], in_=bk[:, 0:npr], func=AF.Identity, scale=cwT[:, fi, 0:1], bias=cwT[:, fi, 3:4]),
                              r=[bk, cwT], w=[tt])
                            E("dve", lambda e, bk=bk, tt=tt, fi=fi, npr=npr: e.scalar_tensor_tensor(
                                out=tt[:, 0:npr], in0=bk[:, 1:1 + npr], scalar=cwT[:, fi, 1:2], in1=tt[:, 0:npr], op0=ALU.mult, op1=ALU.add),
                              r=[bk, cwT, tt], w=[tt])
                            E("dve", lambda e, bk=bk, tt=tt, fi=fi, npr=npr: e.scalar_tensor_tensor(
                                out=tt[:, 0:npr], in0=bk[:, 2:2 + npr], scalar=cwT[:, fi, 2:3], in1=tt[:, 0:npr], op0=ALU.mult, op1=ALU.add),
                              r=[bk, cwT, tt], w=[tt])
                            if n != 512:
                                EB, TS, UL, UP = Eb[half], tS[half], upl[half], ups[half]
                                E("dve", lambda e, EB=EB, fi=fi: e.tensor_copy(
                                    out=EB[:, :, 0:2], in_=cwT[:, fi, 4:12].rearrange("p (s r) -> p s r", s=4)), r=[cwT], w=[EB])
                                E("dve", lambda e, EB=EB, bk=bk: e.tensor_copy(
                                    out=EB[:, :, 2:6], in_=bk[:, 6:22].rearrange("p (s r) -> p s r", s=4)), r=[bk], j=[EB])
                                E("dve", lambda e, UL=UL, bk=bk: e.tensor_copy(out=UL[:, 0:18], in_=bk[:, 4:22]), r=[bk], w=[UL])
                                ts3 = TS[:, :].rearrange("p (s r) -> p s r", s=4)
                                E("act", lambda e, EB=EB, ts3=ts3, fi=fi: e.activation(
                                    out=ts3, in_=EB[:, :, 0:4], func=AF.Identity, scale=cwT[:, fi, 0:1], bias=cwT[:, fi, 3:4]),
                                  r=[EB, cwT], w=[TS])
                                E("dve", lambda e, EB=EB, ts3=ts3, fi=fi: e.scalar_tensor_tensor(
                                    out=ts3, in0=EB[:, :, 1:5], scalar=cwT[:, fi, 1:2], in1=ts3, op0=ALU.mult, op1=ALU.add),
                                  r=[EB, cwT, TS], w=[TS])
                                E("dve", lambda e, EB=EB, ts3=ts3, fi=fi: e.scalar_tensor_tensor(
                                    out=ts3, in0=EB[:, :, 2:6], scalar=cwT[:, fi, 2:3], in1=ts3, op0=ALU.mult, op1=ALU.add),
                                  r=[EB, cwT, TS], w=[TS])
                                bkt = bank()
                                E("pe", lambda e, bkt=bkt, UL=UL: e.transpose(out=bkt[0:18, 0:128], in_=UL[:, 0:18], identity=identf[:]),
                                  r=[UL, identf], w=[bkt])
                                E("act", lambda e, bkt=bkt, UP=UP: e.copy(out=UP[:], in_=bkt[0:18, 0:128]), r=[bkt], w=[UP])
                                D("sp", lambda e, UP=UP, fi=fi: e.dma_start(out=conv_dev.ap[:, fi * 128:(fi + 1) * 128], in_=UP[:]),
                                  r=[UP], j=[conv_dev])
                            res.append(npr)
                        npr = res[0]
                        E("act", lambda e, TG=TG, SG=SG, npr=npr: e.activation(out=SG[:, 0:npr], in_=TG[:, 0:npr], func=AF.Silu),
                          r=[TG], w=[SG])
                        E("pool", lambda e, TU=TU, SG=SG, npr=npr, f=f, a0=a0: e.tensor_tensor(
                            out=actT[:, f, a0:a0 + npr], in0=SG[:, 0:npr], in1=TU[:, 0:npr], op=ALU.mult), r=[SG, TU], j=[actT])
                        if n != 512:
                            E("act", lambda e: e.activation(out=sgS[:], in_=tS[1][:], func=AF.Silu), r=[tS[1]], w=[sgS])
                            E("pool", lambda e, f=f: e.tensor_tensor(out=actT[:, f, 1024:1040], in0=sgS[:], in1=tS[0][:], op=ALU.mult),
                              r=[sgS, tS[0]], j=[actT])
                S.barrier()
                S.replay()
            if stop_after == "E":
                return nc
            with ExitStack() as ph:
                wdn_v = wdn_d.rearrange("(f p) c -> p f c", p=128)
                wd = [T(nc, ph, "wd%d" % i, [128, NF, 256], BF16) for i in range(2)]
                h1s = [T(nc, ph, "h1s%d" % i, [128, 256], F32) for i in range(3)]
                h2s = [T(nc, ph, "h2s%d" % i, [128, 256], F32) for i in range(3)]
                cnt = [0]
                for ng in range(8):
                    W = wd[ng % 2]
                    for fq in range(0, NF, 4):
                        f1 = min(NF, fq + 4)
                        D("pool", lambda e, W=W, ng=ng, fq=fq, f1=f1: e.dma_start(
                            out=W[:, fq:f1, :], in_=wdn_v[:, fq:f1, ng * 256:(ng + 1) * 256]),
                          w=[W] if fq == 0 else (), j=() if fq == 0 else [W])
                    for j in range(9):
                        mi = j + 1
                        M = 128 if j < 8 else 16
                        a0 = 128 * j if j < 8 else 1024
                        k = cnt[0] % 3
                        cnt[0] += 1
                        H1, H2 = h1s[k], h2s[k]
                        D("sp", lambda e, H1=H1, mi=mi, ng=ng: e.dma_start(out=H1[:], in_=g_h1[mi].ap[:, ng * 256:(ng + 1) * 256]),
                          r=[g_h1[mi]], w=[H1])
                        bk = bank()
                        mmgroup(bk[0:M, 0:256], bk, [(actT[:, f, a0:a0 + M], W[:, f, :]) for f in range(NF)], [actT, W])
                        E("dve", lambda e, bk=bk, H1=H1, H2=H2, M=M: e.tensor_tensor(out=H2[0:M, :], in0=bk[0:M, 0:256], in1=H1[0:M, :], op=ALU.add),
                          r=[bk, H1], w=[H2])
                        D("sp", lambda e, H2=H2, mi=mi, ng=ng, M=M: e.dma_start(out=g_h2[mi].ap[0:M, ng * 256:(ng + 1) * 256], in_=H2[0:M, :]),
                          r=[H2], j=[g_h2[mi]])
                S.barrier()
                S.replay()
            if stop_after == "F":
                return nc
            with ExitStack() as ph:
                gfin = T(nc, ph, "gfin", [128, 2048], F32)
                D("sp", lambda e: e.dma_start(out=gfin[:], in_=nfin_d.to_broadcast([128, 2048])), w=[gfin])
                h2t = [T(nc, ph, "h2t%d" % i, [128, 2048], F32) for i in range(2)]
                yt = [T(nc, ph, "yt%d" % i, [128, 2048], F32) for i in range(2)]
                jk = T(nc, ph, "jk", [128, 2048], BF16)
                ssg = [T(nc, ph, "ssg%d" % i, [128, 2], F32) for i in range(2)]
                rsg = [T(nc, ph, "rsg%d" % i, [128, 1], F32) for i in range(2)]
                for j in range(9):
                    mi = j + 1
                    M = 128 if j < 8 else 16
                    b2 = j % 2
                    H2, Y = h2t[b2], yt[b2]
                    D("sp", lambda e, H2=H2, mi=mi, M=M: e.dma_start(out=H2[0:M, :], in_=g_h2[mi].ap[0:M, :]), r=[g_h2[mi]], w=[H2])
                    E("act", lambda e, H2=H2, b2=b2, M=M: e.activation(out=jk[0:M, :], in_=H2[0:M, :], func=AF.Square, accum_out=ssg[b2][0:M, 0:1]),
                      r=[H2], w=[jk, ssg[b2]])
                    E("act", lambda e, b2=b2, M=M: e.activation(out=ssg[b2][0:M, 1:2], in_=ssg[b2][0:M, 0:1], func=AF.Sqrt, scale=1.0 / 2048, bias=EPS),
                      r=[ssg[b2]], w=[ssg[b2]])
                    E("dve", lambda e, b2=b2, M=M: e.reciprocal(out=rsg[b2][0:M, 0:1], in_=ssg[b2][0:M, 1:2]), r=[ssg[b2]], w=[rsg[b2]])
                    E("dve", lambda e, H2=H2, Y=Y, b2=b2, M=M: e.scalar_tensor_tensor(
                        out=Y[0:M, :], in0=H2[0:M, :], scalar=rsg[b2][0:M, 0:1], in1=gfin[0:M, :], op0=ALU.mult, op1=ALU.mult),
                      r=[H2, rsg[b2], gfin], w=[Y])
                    if j < 8:
                        D("sp", lambda e, Y=Y, j=j: e.dma_start(out=y_main.ap[j * 128:(j + 1) * 128, :], in_=Y[:]), r=[Y], j=[y_main])
                    else:
                        D("sp", lambda e, Y=Y: e.dma_start(out=y_smp.ap, in_=Y[0:16, :]), r=[Y], w=[y_smp])
                S.barrier()
                S.replay()
        return nc


_CACHE = {}


def _get_nc(stop_after="Z"):
    if stop_after not in _CACHE:
        _CACHE[stop_after] = build(stop_after)
    return _CACHE[stop_after]


def kernel(x_prompt, x_sample, state_gla, cache_dil_k, cache_dil_v, state_ffn_conv,
           norm_mix, w_in, w_gate_up, b_gate, gla_norm, w_out,
           norm_ffn, w_ffn_up, ffn_conv_w, ffn_conv_b, w_ffn_down, norm_final, _stop_after="Z"):
    f = lambda a: np.ascontiguousarray(np.asarray(a, dtype=np.float32))
    x_prompt, x_sample = f(x_prompt), f(x_sample)
    state_gla, ck, cv, sconv = f(state_gla)[0], f(cache_dil_k)[0], f(cache_dil_v)[0], f(state_ffn_conv)[0]
    shared = dict(norm_mix=f(norm_mix).reshape(1, 2048), w_in=f(w_in)[0], wg=f(w_gate_up)[0],
                  bg=f(b_gate).reshape(1, 640), gnorm=f(gla_norm).reshape(1, 128), w_out=f(w_out)[0],
                  norm_ffn=f(norm_ffn).reshape(1, 2048), w_up=f(w_ffn_up)[0], w_down=f(w_ffn_down)[0],
                  norm_final=f(norm_final).reshape(1, 2048))
    convw = f(ffn_conv_w)[0]
    convb = f(ffn_conv_b).reshape(1, 11008)
    in_maps = []
    for c in range(8):
        b, s = c // 4, c % 4
        start = 1024 * s
        xall = np.zeros((NT * 128, 2048), np.float32)
        lo = start - 3072
        src_lo = max(lo, 0)
        xall[src_lo - lo:4096] = x_prompt[b, src_lo:start + 1024]
        xall[T_SMP * 128:T_SMP * 128 + 16] = x_sample[4 * c:4 * c + 4].reshape(16, 2048)
        m = dict(shared)
        m.update(_host_consts(c))
        m["xall"] = xall
        m["sgla"] = np.ascontiguousarray(state_gla[4 * c:4 * c + 4].reshape(4, 640, 128))
        m["ck"] = np.ascontiguousarray(ck[4 * c:4 * c + 4].reshape(4, 2048, 768))
        m["cv"] = np.ascontiguousarray(cv[4 * c:4 * c + 4].reshape(4, 2048, 768))
        m["convp"] = np.ascontiguousarray(np.concatenate([convw, convb, sconv[4 * c:4 * c + 4].reshape(8, 11008)], axis=0))
        in_maps.append(m)
    nc = _get_nc(_stop_after)
    if _stop_after in ("P", "A"):
        for m in in_maps:
            m.pop("w_up"); m.pop("w_down")
    res = run_bass_kernel_spmd(nc, in_maps, core_ids=list(range(8)))
    R = res.results
    y_prompt = np.zeros((2, 4096, 2048), np.float32)
    y_sample = np.zeros((32, 4, 2048), np.float32)
    gla_p = np.zeros((1, 2, 10, 64, 128), np.float32)
    gla_s = np.zeros((1, 32, 10, 64, 128), np.float32)
    k_p = np.zeros((1, 2, 2048, 12, 64), np.float32)
    v_p = np.zeros((1, 2, 2048, 12, 64), np.float32)
    k_s = np.zeros((1, 32, 2048, 12, 64), np.float32)
    v_s = np.zeros((1, 32, 2048, 12, 64), np.float32)
    c_p = np.zeros((1, 2, 2, 11008), np.float32)
    c_s = np.zeros((1, 32, 2, 11008), np.float32)

    def unstate(a):
        return a.reshape(2, 64, 5, 128).transpose(2, 0, 1, 3).reshape(10, 64, 128)
    for c in range(8):
        b, s = c // 4, c % 4
        r = R[c]
        y_prompt[b, 1024 * s:1024 * s + 1024] = r["y_main"]
        y_sample[4 * c:4 * c + 4] = r["y_smp"].reshape(4, 4, 2048)
        if s == 3:
            gla_p[0, b] = unstate(r["gla_end"])
            c_p[0, b] = r["conv_dev"][0:2]
        for sq in range(4):
            gla_s[0, 4 * c + sq] = unstate(r["gla_smp"][sq])
            k_s[0, 4 * c + sq] = r["ks_out"][sq].reshape(2048, 12, 64)
            v_s[0, 4 * c + sq] = r["vs_out"][sq].reshape(2048, 12, 64)
            c_s[0, 4 * c + sq] = r["conv_dev"][2 + 4 * sq + 2:2 + 4 * sq + 4]
        if s >= 2:
            k_p[0, b, (s - 2) * 1024:(s - 1) * 1024] = r["k_main"].reshape(1024, 12, 64)
            v_p[0, b, (s - 2) * 1024:(s - 1) * 1024] = r["v_main"].reshape(1024, 12, 64)
    return (y_prompt, y_sample, gla_p, gla_s, k_p, k_s, v_p, v_s, c_p, c_s)
```

```python
import numpy as np

from contextlib import ExitStack
import concourse.bass as bass
import concourse.mybir as mybir
from concourse.bass_utils import run_bass_kernel_spmd

F32 = mybir.dt.float32
BF16 = mybir.dt.bfloat16
ALU = mybir.AluOpType
AF = mybir.ActivationFunctionType
AX = mybir.AxisListType
ENGS = ("pe", "act", "dve", "pool", "sp")
EPS = 1e-6
NT = 34
T_SMP = 32
NKB = 26
D_FF = 5504
NF = 43


class Res:
    __slots__ = ("name", "w", "r", "pr")

    def __init__(self, name=""):
        self.name = name
        self.w = {}
        self.r = {}
        self.pr = {}


class Sched:
    def __init__(self, nc, stack, n_dma_sems=32):
        self.nc = nc
        self.q = {e: [] for e in ENGS}
        self.cnt = {e: 0 for e in ENGS}
        self.waited = {e: {} for e in ENGS}
        self.sem = {}
        for e in ENGS:
            self.sem[e] = stack.enter_context(nc.semaphore("s_" + e))
        self.ndma = n_dma_sems
        self.dtot = [0] * n_dma_sems
        for k in range(n_dma_sems):
            self.sem[("d", k)] = stack.enter_context(nc.semaphore("s_d%d" % k))
        self.dnext = 0
        self.dnext_sw = 0

    def _deps(self, eng, reads, writes, extra=None):
        d = {}

        def add(k, v):
            if v > d.get(k, 0):
                d[k] = v
        for r in reads:
            for k, v in r.w.items():
                add(k, v)
        for w in writes:
            for k, v in w.w.items():
                add(k, v)
            for k, v in w.r.items():
                add(k, v)
        if extra:
            for k, v in extra:
                add(k, v)
        wl = []
        wd = self.waited[eng]
        for k, v in d.items():
            if v > wd.get(k, 0):
                wd[k] = v
                wl.append((k, v))
        return wl

    def _mark(self, tok, reads, writes, joins):
        for r in reads:
            if tok[1] > r.r.get(tok[0], 0):
                r.r[tok[0]] = tok[1]
        for w in writes:
            w.pr = dict(w.r)
            w.w = {tok[0]: tok[1]}
            w.r = {}
        for w in joins:
            w.w[tok[0]] = max(w.w.get(tok[0], 0), tok[1])

    def emit(self, eng, fn, reads=(), writes=(), signal=True, joins=(), extra=None):
        wl = self._deps(eng, reads, list(writes), extra)
        if joins:
            d2 = {}
            for w in joins:
                for src in (w.r, w.pr):
                    for k, v in src.items():
                        if v > d2.get(k, 0):
                            d2[k] = v
            wd = self.waited[eng]
            for k, v in d2.items():
                if v > wd.get(k, 0):
                    wd[k] = v
                    wl.append((k, v))
        if signal:
            self.cnt[eng] += 1
            tok = (eng, self.cnt[eng])
        else:
            tok = (eng, self.cnt[eng] + 1)
        self.q[eng].append((wl, fn, 1 if signal else 0, None))
        self._mark(tok, reads, writes, joins)
        return tok

    def dma(self, eng, fn, reads=(), writes=(), joins=()):
        half = self.ndma // 2
        if eng == "pool":
            k = half + (self.dnext_sw % (self.ndma - half))
            self.dnext_sw += 1
        else:
            k = self.dnext % half
            self.dnext += 1
        key = ("d", k)
        extra = [(key, self.dtot[k])] if self.dtot[k] > 0 else None
        wl = self._deps(eng, reads, list(writes), extra)
        if joins:
            wd = self.waited[eng]
            for w in joins:
                for src in (w.r, w.pr):
                    for kk, v in src.items():
                        if v > wd.get(kk, 0):
                            wd[kk] = v
                            wl.append((kk, v))
        self.dtot[k] += 16
        tok = (key, self.dtot[k])
        self.q[eng].append((wl, fn, 16, key))
        self._mark(tok, reads, writes, joins)
        return tok

    def barrier(self):
        tot = [(("d", k), self.dtot[k]) for k in range(self.ndma) if self.dtot[k] > 0]
        tot += [(e, self.cnt[e]) for e in ENGS if self.cnt[e] > 0]
        for e in ENGS:
            wl = []
            wd = self.waited[e]
            for k, v in tot:
                if v > wd.get(k, 0):
                    wd[k] = v
                    wl.append((k, v))
            if wl:
                self.q[e].append((wl, None, 0, None))

    def replay(self):
        nc = self.nc
        with nc.Block() as block:
            deco = {"pe": block.tensor, "act": block.scalar, "dve": block.vector,
                    "pool": block.gpsimd, "sp": block.sync}
            for e in ENGS:
                items = self.q[e]
                if not items:
                    continue

                def body(engine, items=items, e=e):
                    for wl, fn, inc, key in items:
                        for k, v in wl:
                            engine.wait_ge(self.sem[k], v)
                        if fn is None:
                            continue
                        ins = fn(engine)
                        if inc == 1:
                            ins.then_inc(self.sem[e], 1)
                        elif inc == 16:
                            ins.then_inc(self.sem[key], 16)
                deco[e](body)
        self.q = {e: [] for e in ENGS}


class T:
    def __init__(self, nc, stack, name, shape, dtype, psum=False):
        if psum:
            self.t = stack.enter_context(nc.psum_tensor("t_" + name, shape, dtype))
        else:
            self.t = stack.enter_context(nc.sbuf_tensor("t_" + name, shape, dtype))
        self.r = Res(name)
        self.psum = psum

    def __getitem__(self, k):
        return self.t[k]


class DT:
    def __init__(self, ap, name=""):
        self.ap = ap
        self.r = Res(name)

    def __getitem__(self, k):
        return self.ap[k]


def _cmult(d):
    d = np.asarray(d)
    ok = d >= 0
    return ((ok & (d <= 128)).astype(np.float32)
            + (ok & (d <= 512) & (d % 4 == 0)).astype(np.float32)
            + (ok & (d <= 2048) & (d % 16 == 0)).astype(np.float32))


def _host_consts(c):
    s = c % 4
    start = 1024 * s
    half = 32
    inv_freq = (np.float32(10000.0) ** (-2.0 * np.arange(half, dtype=np.float32) / 64)).astype(np.float32)
    pos = np.zeros(NT * 128, np.float32)
    for t in range(32):
        pos[t * 128:(t + 1) * 128] = start - 3072 + 128 * t + np.arange(128)
    for p in range(16):
        pos[T_SMP * 128 + p] = 8192 + p % 4
    ang = pos[:, None].astype(np.float32) * inv_freq[None, :]
    cs = np.concatenate([np.cos(ang), np.sin(ang)], axis=1).astype(np.float32)
    flags = np.zeros((128, 32), np.float32)
    for kb in range(25):
        t = kb + 7
        flags[:, kb] = 1.0 if (start - 3072 + 128 * t) >= 0 else 0.0
    flags[:, 25] = 1.0
    flags[:, 26] = 1.0 if s > 0 else 0.0
    p = np.arange(128)[:, None]
    q = np.arange(128)[None, :]
    mask_p = np.zeros((128, 17 * 128), np.float32)
    for i in range(17):
        dl = 16 - i
        mask_p[:, i * 128:(i + 1) * 128] = _cmult(128 * dl + q - p)
    mask_s = np.zeros((128, 4 * 16 * 16 + 16), np.float32)
    qc = np.arange(16)[None, :]
    for sq in range(4):
        for blk in range(16):
            j = 128 * blk + p
            m = _cmult(2048 + (qc % 4) - j) * ((qc // 4) == sq)
            mask_s[:, (sq * 16 + blk) * 16:(sq * 16 + blk) * 16 + 16] = m
    pn = np.arange(128)[:, None]
    mn = _cmult((qc % 4) - (pn % 4)) * ((qc // 4) == (pn // 4)) * (pn < 16)
    mask_s[:, 1024:1040] = mn
    gm = np.zeros((128, 776), np.float32)
    same = (p // 64) == (q // 64)
    gm[:, 0:128] = np.where(same & (p <= q), -1.0 / 16, 0.0)
    gm[:, 128:256] = np.where(same & (p > q), -1.0 / 16, 0.0)
    gm[:, 256:384] = np.where(same & (p <= q), 1.0, 0.0)
    sames = ((p // 4) == (q // 4)) & (p < 16) & (q < 16)
    gm[:, 384:512] = np.where(sames & (p <= q), -1.0 / 16, 0.0)
    gm[:, 512:640] = np.where(sames & (p > q), -1.0 / 16, 0.0)
    gm[:, 640:768] = np.where(sames & (p <= q), 1.0, 0.0)
    for ch in range(2):
        gm[:, 768 + ch] = np.where((np.arange(128) // 64) == ch, -1.0 / 16, 0.0)
    for sq in range(4):
        gm[:, 772 + sq] = np.where((np.arange(128) < 16) & ((np.arange(128) // 4) == sq), -1.0 / 16, 0.0)
    smask = np.zeros((128, 68), np.float32)
    for sq in range(4):
        smask[:, sq * 16:(sq + 1) * 16] = ((np.arange(16) // 4) == sq).astype(np.float32)[None, :]
        smask[:, 64 + sq] = ((np.arange(128) < 16) & ((np.arange(128) // 4) == sq)).astype(np.float32)
    return dict(cs=cs, flags=flags, mask_p=mask_p, mask_s=mask_s, gm=gm, smask=smask)


def build(stop_after="Z"):
    nc = bass.Bass("TRN2", target_bir_lowering=False)

    def din(name, shape):
        return nc.dram_tensor(name, list(shape), F32, kind="ExternalInput").ap()

    def dout(name, shape):
        return DT(nc.dram_tensor(name, list(shape), F32, kind="ExternalOutput").ap(), name)

    def dscr(name, shape, dt):
        return DT(nc.dram_tensor(name, list(shape), dt).ap(), name)

    xall = din("xall", [NT * 128, 2048])
    cs_d = din("cs", [NT * 128, 64])
    flags_d = din("flags", [128, 32])
    maskp_d = din("mask_p", [128, 17 * 128])
    masks_d = din("mask_s", [128, 1040])
    gm_d = din("gm", [128, 776])
    smask_d = din("smask", [128, 68])
    sgla_d = din("sgla", [4, 640, 128])
    ck_d = din("ck", [4, 2048, 768])
    cv_d = din("cv", [4, 2048, 768])
    convp_d = din("convp", [12, 11008])
    nmix_d = din("norm_mix", [1, 2048])
    win_d = din("w_in", [2048, 6160])
    wg_d = din("wg", [16, 640])
    bg_d = din("bg", [1, 640])
    gn_d = din("gnorm", [1, 128])
    wout_d = din("w_out", [2048, 2048])
    nffn_d = din("norm_ffn", [1, 2048])
    if stop_after not in ("P", "A"):
        wup_d = din("w_up", [2048, 11008])
        wdn_d = din("w_down", [5504, 2048])
    nfin_d = din("norm_final", [1, 2048])

    y_main = dout("y_main", [1024, 2048])
    y_smp = dout("y_smp", [16, 2048])
    gla_end = dout("gla_end", [128, 640])
    gla_smp = dout("gla_smp", [4, 128, 640])
    k_main = dout("k_main", [1024, 768])
    v_main = dout("v_main", [1024, 768])
    ks_out = dout("ks_out", [4, 2048, 768])
    vs_out = dout("vs_out", [4, 2048, 768])
    conv_dev = dout("conv_dev", [18, 11008])

    xTs = [dscr("xTs%d" % i, [128, 2048], BF16) for i in range(10)]
    g_kT = [dscr("g_kT%d" % i, [128, 640], BF16) for i in range(10)]
    g_v = [dscr("g_v%d" % i, [128, 1280], BF16) for i in range(10)]
    g_S = [dscr("g_S%d" % i, [128, 4 * 640], BF16) for i in range(10)]
    g_eq = [dscr("g_eq%d" % i, [128, 640], F32) for i in range(10)]
    g_qT = [dscr("g_qT%d" % i, [128, 640], BF16) for i in range(10)]
    g_gate = [dscr("g_gate%d" % i, [128, 1280], BF16) for i in range(10)]
    g_QT = [dscr("g_QT%d" % i, [128, 768], BF16) for i in range(10)]
    g_oT = [dscr("g_oT%d" % i, [128, 2048], BF16) for i in range(10)]
    g_h1 = [dscr("g_h1%d" % i, [128, 2048], F32) for i in range(10)]
    g_h2 = [dscr("g_h2%d" % i, [128, 2048], F32) for i in range(10)]
    KTs = [dscr("KTs%d" % i, [128, 768], BF16) for i in range(NKB)]
    Vs = [dscr("Vs%d" % i, [128, 780], BF16) for i in range(NKB)]

    with ExitStack() as G:
        S = Sched(nc, G)

        def E(eng, fn, r=(), w=(), j=(), signal=True):
            rr = [x.r for x in r if not getattr(x, "psum", False)]
            ww = [x.r for x in w] + [x.r for x in r if getattr(x, "psum", False)]
            extra = []
            for x in j:
                if getattr(x, "psum", False):
                    for k, v in x.r.w.items():
                        if k != eng:
                            extra.append((k, v))
            return S.emit(eng, fn, rr, ww, signal=signal, joins=[x.r for x in j], extra=extra or None)

        def D(eng, fn, r=(), w=(), j=()):
            return S.dma(eng, fn, [x.r for x in r], [x.r for x in w], joins=[x.r for x in j])

        pb = [T(nc, G, "pb%d" % i, [128, 512], F32, psum=True) for i in range(6)]
        pbf = [T(nc, G, "pbf%d" % i, [128, 1024], BF16, psum=True) for i in range(2)]
        bi = [0, 0]

        reserved = []

        def bank():
            while True:
                b = pb[bi[0] % 6]
                bi[0] += 1
                if b not in reserved:
                    return b

        def bankbf():
            b = pbf[bi[1] % 2]
            bi[1] += 1
            return b

        ident = T(nc, G, "ident", [128, 128], BF16)
        identf = T(nc, G, "identf", [128, 128], F32)
        gm = T(nc, G, "gm", [128, 776], F32)
        flags = T(nc, G, "flags", [128, 32], F32)
        smask = T(nc, G, "smask", [128, 68], F32)
        cwT = T(nc, G, "cwT", [128, 86, 12], F32)
        Sst = T(nc, G, "Sst", [128, 640], F32)

        for idt in (ident, identf):
            E("pool", lambda e, t=idt: e.memset(t[:], 0.0), w=[idt])
            E("pool", lambda e, t=idt: e.affine_select(out=t[:], in_=t[:], pattern=[[-1, 128]],
                                                       compare_op=ALU.not_equal, fill=1.0, base=0,
                                                       channel_multiplier=1), r=[idt], w=[idt])
        D("sp", lambda e: e.dma_start(out=gm[:], in_=gm_d), w=[gm])
        D("sp", lambda e: e.dma_start(out=flags[:], in_=flags_d), w=[flags])
        D("sp", lambda e: e.dma_start(out=smask[:], in_=smask_d), w=[smask])
        E("dve", lambda e: e.memset(Sst[:], 0.0), w=[Sst])

        def mmgroup(out_ap, outT, pairs, reads, fresh=True, first=True, last=True):
            n = len(pairs)
            for i, (l, r_) in enumerate(pairs):
                fn = (lambda e, l=l, r_=r_, st=(first and i == 0), sp_=(last and i == n - 1):
                      e.matmul(out_ap, lhsT=l, rhs=r_, start=st, stop=sp_))
                if i == 0 and fresh:
                    E("pe", fn, r=reads, w=[outT], signal=(i == n - 1))
                else:
                    E("pe", fn, r=reads, j=[outT], signal=(i == n - 1))

        def transposes(srcs, dst_bank, width, reads, idn):
            n = len(srcs)
            for i, (a, rows) in enumerate(srcs):
                fn = (lambda e, a=a, i=i, rows=rows:
                      e.transpose(out=dst_bank[0:128, i * width:i * width + rows], in_=a, identity=idn[0:rows, 0:rows]))
                if i == 0:
                    E("pe", fn, r=reads + [idn], w=[dst_bank], signal=(i == n - 1))
                else:
                    E("pe", fn, r=reads + [idn], j=[dst_bank], signal=(i == n - 1))

        def rms_rstd(x_ap, junk, ss, rstd, xr, n):
            E("act", lambda e: e.activation(out=junk[:, 0:n], in_=x_ap, func=AF.Square, accum_out=ss[:, 0:1]),
              r=[xr], w=[junk, ss])
            E("act", lambda e: e.activation(out=ss[:, 1:2], in_=ss[:, 0:1], func=AF.Sqrt, scale=1.0 / n, bias=EPS),
              r=[ss], w=[ss])
            E("dve", lambda e: e.reciprocal(out=rstd[:, 0:1], in_=ss[:, 1:2]), r=[ss], w=[rstd])

        with ExitStack() as ph:
            cst = T(nc, ph, "cst", [12, 11008], F32)
            D("sp", lambda e: e.dma_start(out=cst[:], in_=convp_d), w=[cst])
            import os as _os0
            for f0 in range(0, 0 if _os0.environ.get("KSKIP_P") else 86, 4):
                nf = min(4, 86 - f0)
                bk = bank()
                for i in range(nf):
                    f = f0 + i
                    fn = (lambda e, f=f, i=i, bk=bk:
                          e.transpose(out=bk[0:128, i * 12:i * 12 + 12], in_=cst[0:12, f * 128:(f + 1) * 128],
                                      identity=identf[0:12, 0:12]))
                    if i == 0:
                        E("pe", fn, r=[cst, identf], w=[bk], signal=(i == nf - 1))
                    else:
                        E("pe", fn, r=[cst, identf], j=[bk], signal=(i == nf - 1))
                E("act", lambda e, bk=bk, f0=f0, nf=nf: e.copy(
                    out=cwT[:, f0:f0 + nf, :], in_=bk[:, 0:nf * 12].rearrange("p (a b) -> p a b", a=nf)),
                  r=[bk], j=[cwT])
            S.barrier()
            S.replay()

        if stop_after == "P":
            return nc
        win_v = win_d.rearrange("(kc p) c -> p kc c", p=128)
        with ExitStack() as ph:
            WA = T(nc, ph, "WA", [128, 16, 3472], BF16)
            rngs = [(3840, 16, 0), (640, 640, 16), (1280, 1280, 656), (4624, 768, 1936), (5392, 768, 2704)]
            for (c0, w_, off) in rngs:
                for kq in range(4):
                    D("pool", lambda e, c0=c0, w_=w_, off=off, kq=kq: e.dma_start(
                        out=WA[:, 4 * kq:4 * kq + 4, off:off + w_], in_=win_v[:, 4 * kq:4 * kq + 4, c0:c0 + w_]),
                      j=[WA])
            for sq in range(4):
                for hh in range(2):
                    r0, r1 = (0, 1024) if hh == 0 else (1024, 2044)
                    D("pool", lambda e, sq=sq, r0=r0, r1=r1: e.dma_start(out=ks_out.ap[sq, r0:r1, :], in_=ck_d[sq, r0 + 4:r1 + 4, :]), j=[ks_out])
                    D("pool", lambda e, sq=sq, r0=r0, r1=r1: e.dma_start(out=vs_out.ap[sq, r0:r1, :], in_=cv_d[sq, r0 + 4:r1 + 4, :]), j=[vs_out])
            gmix = T(nc, ph, "gmix", [128, 2048], F32)
            D("sp", lambda e: e.dma_start(out=gmix[:], in_=nmix_d.to_broadcast([128, 2048])), w=[gmix])
            bgb = T(nc, ph, "bgb", [128, 640], F32)
            D("sp", lambda e: e.dma_start(out=bgb[:], in_=bg_d.to_broadcast([128, 640])), w=[bgb])
            wgt = T(nc, ph, "wgt", [16, 640], BF16)
            D("pool", lambda e: e.dma_start(out=wgt[:], in_=wg_d), w=[wgt])
            gmb = T(nc, ph, "gmb", [128, 776], BF16)
            E("dve", lambda e: e.tensor_copy(out=gmb[:], in_=gm[:]), r=[gm], w=[gmb])
            spbb = T(nc, ph, "spbb", [128, 640], BF16)
            xt = [T(nc, ph, "xt0", [128, 2048], F32)] * 2
            xn = T(nc, ph, "xn", [128, 2048], BF16)
            junk = xn
            xT = [T(nc, ph, "xT%d" % i, [128, 16, 128], BF16) for i in range(2)]
            ss = [T(nc, ph, "ss%d" % i, [128, 2], F32) for i in range(2)]
            rstd = [T(nc, ph, "rstd%d" % i, [128, 1], F32) for i in range(2)]
            cst_ = [T(nc, ph, "cs%d" % i, [128, 64], F32) for i in range(2)]
            ksb = T(nc, ph, "ksb", [128, 640], F32)
            zsb = T(nc, ph, "zsb", [128, 16], F32)
            zT = T(nc, ph, "zT", [16, 128], BF16)
            usb = T(nc, ph, "usb", [128, 640], F32)
            spb = T(nc, ph, "spb", [128, 640], F32)
            kdec = T(nc, ph, "kdec", [128, 640], F32)
            khat = T(nc, ph, "khat", [128, 640], BF16)
            eqb = kdec
            ekb = usb
            ktl = T(nc, ph, "ktl", [128, 640], BF16)
            kTt = T(nc, ph, "kTt", [128, 640], BF16)
            Dsb = T(nc, ph, "Dsb", [128, 20], F32)
            vbf = [T(nc, ph, "vbf%d" % i, [128, 1280], BF16) for i in range(2)]
            Sbf = T(nc, ph, "Sbf", [128, 4 * 640], BF16)
            kb32 = T(nc, ph, "kb32", [128, 768], F32)
            krot = [T(nc, ph, "krot0", [128, 768], F32)] * 2
            rt = [T(nc, ph, "rt%d" % i, [128, 384], F32) for i in range(4)]
            kbf = T(nc, ph, "kbf", [128, 768], BF16)
            KTt = [T(nc, ph, "KTt%d" % i, [128, 768], BF16) for i in range(2)]
            vsb = [T(nc, ph, "vsb0", [128, 768], F32)] * 2
            vaug = [T(nc, ph, "vaug%d" % i, [128, 12, 65], BF16) for i in range(2)]
            S0q = [kdec] * 2
            S1q = [usb] * 2
            for va in vaug:
                E("pool", lambda e, va=va: e.memset(va[:], 1.0), w=[va])

            tiles = list(range(32)) + [T_SMP]
            import os as _os
            if _os.environ.get("KTILES"):
                tiles = [int(x) for x in _os.environ["KTILES"].split(",")]
            KSEC = int(_os.environ.get("KSEC", "9"))
            KSUB = int(_os.environ.get("KSUB", "9"))
            for it, t in enumerate(tiles):
                b2 = it % 2
                mixer = (t >= 23)
                mi = (t - 23) if t < 32 else 9
                smp = (t == T_SMP)
                halo = (t >= 7)
                kb = (t - 7) if t < 32 else 25
                X = xt[b2]
                D("sp", lambda e, X=X, t=t: e.dma_start(out=X[:], in_=xall[t * 128:(t + 1) * 128, :]), w=[X])
                if halo:
                    CS = cst_[b2]
                    D("sp", lambda e, CS=CS, t=t: e.dma_start(out=CS[:], in_=cs_d[t * 128:(t + 1) * 128, :]), w=[CS])
                rms_rstd(X[:], junk, ss[b2], rstd[b2], X, 2048)
                E("dve", lambda e, X=X, b2=b2: e.scalar_tensor_tensor(
                    out=xn[:], in0=X[:], scalar=rstd[b2][:, 0:1], in1=gmix[:], op0=ALU.mult, op1=ALU.mult),
                  r=[X, rstd[b2], gmix], w=[xn])
                XT = xT[b2]
                for q4 in range(2):
                    bk = bankbf()
                    transposes([(xn[:, (q4 * 8 + i) * 128:(q4 * 8 + i + 1) * 128], 128) for i in range(8)],
                               bk, 128, [xn], ident)
                    eng = "act" if q4 == 0 else "dve"
                    if eng == "act":
                        E("act", lambda e, bk=bk, XT=XT, q4=q4: e.copy(
                            out=XT[:, q4 * 8:q4 * 8 + 8, :], in_=bk[:, :].rearrange("p (a b) -> p a b", a=8)),
                          r=[bk], j=[XT] if q4 else (), w=[XT] if not q4 else ())
                    else:
                        E("dve", lambda e, bk=bk, XT=XT, q4=q4: e.tensor_copy(
                            out=XT[:, q4 * 8:q4 * 8 + 8, :], in_=bk[:, :].rearrange("p (a b) -> p a b", a=8)),
                          r=[bk], j=[XT])
                if mixer:
                    D("sp", lambda e, XT=XT, mi=mi: e.dma_start(
                        out=xTs[mi].ap.rearrange("p (a b) -> p a b", a=16), in_=XT[:]), r=[XT], w=[xTs[mi]])

                def proj(off, width):
                    res = []
                    c = 0
                    while c < width:
                        n = min(512, width - c)
                        bk = bank()
                        mmgroup(bk[:, 0:n], bk,
                                [(XT[:, kc, :], WA[:, kc, off + c:off + c + n]) for kc in range(16)], [XT, WA])
                        res.append((bk, n, c))
                        c += n
                    return res

                if KSEC < 2:
                    continue
                pz = proj(0, 656)
                (b0, n0, _), (b1, n1, _) = pz
                if KSUB < 1:
                    continue
                E("dve", lambda e, b0=b0: e.tensor_copy(out=zsb[:], in_=b0[:, 0:16]), r=[b0], w=[zsb])
                E("act", lambda e, b0=b0: e.copy(out=ksb[:, 0:496], in_=b0[:, 16:512]), r=[b0], w=[ksb])
                E("act", lambda e, b1=b1: e.copy(out=ksb[:, 496:640], in_=b1[:, 0:144]), r=[b1], j=[ksb])
                VB = vbf[b2]
                pv = proj(656, 1280)
                for i, (bk, n, c) in enumerate(pv):
                    if i % 2 == 0:
                        E("act", lambda e, bk=bk, n=n, c=c, VB=VB: e.copy(out=VB[:, c:c + n], in_=bk[:, 0:n]),
                          r=[bk], w=[VB] if i == 0 else (), j=() if i == 0 else [VB])
                    else:
                        E("dve", lambda e, bk=bk, n=n, c=c, VB=VB: e.tensor_copy(out=VB[:, c:c + n], in_=bk[:, 0:n]),
                          r=[bk], j=[VB])
                if mixer:
                    D("sp", lambda e, mi=mi, VB=VB: e.dma_start(out=g_v[mi].ap, in_=VB[:]), r=[VB], w=[g_v[mi]])
                if KSUB < 2:
                    continue
                bk = bank()
                E("pe", lambda e, bk=bk: e.transpose(out=bk[0:16, 0:128], in_=zsb[:, 0:16], identity=identf[:]),
                  r=[zsb, identf], w=[bk])
                E("act", lambda e, bk=bk: e.copy(out=zT[:], in_=bk[0:16, 0:128]), r=[bk], w=[zT])
                ba = bank()
                bb = bank()
                mmgroup(ba[:, 0:512], ba, [(zT[:, :], wgt[:, 0:512])], [zT, wgt])
                mmgroup(bb[:, 0:128], bb, [(zT[:, :], wgt[:, 512:640])], [zT, wgt])
                E("dve", lambda e, ba=ba: e.tensor_tensor(out=usb[:, 0:512], in0=ba[:, 0:512], in1=bgb[:, 0:512], op=ALU.add),
                  r=[ba, bgb], w=[usb])
                E("dve", lambda e, bb=bb: e.tensor_tensor(out=usb[:, 512:640], in0=bb[:, 0:128], in1=bgb[:, 512:640], op=ALU.add),
                  r=[bb, bgb], j=[usb])
                E("act", lambda e: e.activation(out=usb[:], in_=usb[:], func=AF.Exp, scale=-1.0), r=[usb], w=[usb])
                E("act", lambda e: e.activation(out=spb[:], in_=usb[:], func=AF.Ln, bias=1.0), r=[usb], w=[spb])
                E("dve", lambda e: e.tensor_copy(out=spbb[:], in_=spb[:]), r=[spb], w=[spbb])
                if halo:
                    VS = vsb[b2]
                    pk = proj(1936, 768)
                    for i, (bk, n, c) in enumerate(pk):
                        E("act", lambda e, bk=bk, n=n, c=c: e.copy(out=kb32[:, c:c + n], in_=bk[:, 0:n]),
                          r=[bk], w=[kb32] if i == 0 else (), j=() if i == 0 else [kb32])
                    pvb = proj(2704, 768)
                    for i, (bk, n, c) in enumerate(pvb):
                        E("dve", lambda e, bk=bk, n=n, c=c, VS=VS: e.tensor_copy(out=VS[:, c:c + n], in_=bk[:, 0:n]),
                          r=[bk], w=[VS] if i == 0 else (), j=() if i == 0 else [VS])
                if KSUB < 3:
                    continue
                go = 384 if smp else 0
                ba = bank()
                bb = bank()
                mmgroup(ba[:, 0:512], ba, [(gmb[:, go + 128:go + 256], spbb[:, 0:512])], [gmb, spbb])
                mmgroup(bb[:, 0:128], bb, [(gmb[:, go + 128:go + 256], spbb[:, 512:640])], [gmb, spbb])
                E("act", lambda e, ba=ba: e.activation(out=kdec[:, 0:512], in_=ba[:, 0:512], func=AF.Exp), r=[ba], w=[kdec])
                E("act", lambda e, bb=bb: e.activation(out=kdec[:, 512:640], in_=bb[:, 0:128], func=AF.Exp), r=[bb], j=[kdec])
                E("dve", lambda e: e.tensor_tensor(out=khat[:], in0=ksb[:], in1=kdec[:], op=ALU.mult),
                  r=[ksb, kdec], w=[khat])
                if KSUB < 4:
                    continue
                nch = 4 if smp else 2
                io = 772 if smp else 768
                bk = bank()
                for m in range(5):
                    mmgroup(bk[:, m * nch:(m + 1) * nch], bk, [(spbb[:, m * 128:(m + 1) * 128], gmb[:, io:io + nch])],
                            [spbb, gmb], fresh=(m == 0))
                E("act", lambda e, bk=bk, nch=nch: e.activation(out=Dsb[:, 0:5 * nch], in_=bk[:, 0:5 * nch], func=AF.Exp),
                  r=[bk], w=[Dsb])
                if mixer and KSEC >= 3:
                    ba = bank()
                    bb = bank()
                    mmgroup(ba[:, 0:512], ba, [(gmb[:, go:go + 128], spbb[:, 0:512])], [gmb, spbb])
                    mmgroup(bb[:, 0:128], bb, [(gmb[:, go:go + 128], spbb[:, 512:640])], [gmb, spbb])
                    E("act", lambda e, ba=ba: e.activation(out=eqb[:, 0:512], in_=ba[:, 0:512], func=AF.Exp), r=[ba], w=[eqb])
                    E("act", lambda e, bb=bb: e.activation(out=eqb[:, 512:640], in_=bb[:, 0:128], func=AF.Exp), r=[bb], j=[eqb])
                    E("act", lambda e, ba=ba: e.activation(out=ekb[:, 0:512], in_=ba[:, 0:512], func=AF.Exp, scale=-1.0), r=[ba], w=[ekb])
                    E("act", lambda e, bb=bb: e.activation(out=ekb[:, 512:640], in_=bb[:, 0:128], func=AF.Exp, scale=-1.0), r=[bb], j=[ekb])
                    D("sp", lambda e, mi=mi: e.dma_start(out=g_eq[mi].ap, in_=eqb[:]), r=[eqb], w=[g_eq[mi]])
                    E("dve", lambda e: e.tensor_tensor(out=ktl[:], in0=ksb[:], in1=ekb[:], op=ALU.mult), r=[ksb, ekb], w=[ktl])
                    bkf = bankbf()
                    transposes([(ktl[:, m * 128:(m + 1) * 128], 128) for m in range(5)], bkf, 128, [ktl], ident)
                    E("act", lambda e, bkf=bkf: e.copy(out=kTt[:], in_=bkf[:, 0:640]), r=[bkf], w=[kTt])
                    D("sp", lambda e, mi=mi: e.dma_start(out=g_kT[mi].ap, in_=kTt[:]), r=[kTt], w=[g_kT[mi]])
                if KSEC < 5:
                    continue
                if not smp:
                    for ch in range(2):
                        if mixer:
                            E("act", lambda e, ch=ch: e.copy(out=Sbf[:, ch * 640:(ch + 1) * 640], in_=Sst[:]),
                              r=[Sst], w=[Sbf] if ch == 0 else (), j=() if ch == 0 else [Sbf])
                        ba = bank()
                        bb = bank()
                        for m in range(5):
                            tgt = ba if m < 4 else bb
                            mc = (m % 4) * 128
                            for h2 in range(2):
                                h = 2 * m + h2
                                mmgroup(tgt[64 * h2:64 * h2 + 64, mc:mc + 128], tgt,
                                        [(khat[64 * ch:64 * ch + 64, h * 64:(h + 1) * 64],
                                          VB[64 * ch:64 * ch + 64, h * 128:(h + 1) * 128])],
                                        [khat, VB], fresh=(m % 4 == 0 and h2 == 0))
                        for m in range(5):
                            tgt = ba if m < 4 else bb
                            mc = (m % 4) * 128
                            E("dve", lambda e, m=m, tgt=tgt, mc=mc, ch=ch: e.scalar_tensor_tensor(
                                out=Sst[:, m * 128:(m + 1) * 128], in0=Sst[:, m * 128:(m + 1) * 128],
                                scalar=Dsb[:, 2 * m + ch:2 * m + ch + 1], in1=tgt[:, mc:mc + 128],
                                op0=ALU.mult, op1=ALU.add), r=[Sst, Dsb, tgt], w=[Sst])
                    if mixer:
                        D("sp", lambda e, mi=mi: e.dma_start(out=g_S[mi].ap[:, 0:1280], in_=Sbf[:, 0:1280]),
                          r=[Sbf], w=[g_S[mi]])
                    if t == 31:
                        D("sp", lambda e: e.dma_start(out=gla_end.ap, in_=Sst[:]), r=[Sst], w=[gla_end])
                else:
                    for sq in range(4):
                        S0 = S0q[sq % 2]
                        S1 = S1q[sq % 2]
                        D("sp", lambda e, sq=sq, S0=S0: e.dma_start(
                            out=S0[:, :].rearrange("p (m v) -> p m v", m=5),
                            in_=sgla_d[sq].rearrange("(m p) v -> p m v", p=128)), w=[S0])
                        E("act", lambda e, sq=sq, S0=S0: e.copy(out=Sbf[:, sq * 640:(sq + 1) * 640], in_=S0[:]),
                          r=[S0], w=[Sbf] if sq == 0 else (), j=() if sq == 0 else [Sbf])
                        E("dve", lambda e, sq=sq: e.tensor_scalar(
                            out=ktl[:], in0=khat[:], scalar1=smask[:, 64 + sq:65 + sq], scalar2=None, op0=ALU.mult),
                          r=[khat, smask], w=[ktl])
                        ba = bank()
                        bb = bank()
                        for m in range(5):
                            tgt = ba if m < 4 else bb
                            mc = (m % 4) * 128
                            for h2 in range(2):
                                h = 2 * m + h2
                                mmgroup(tgt[64 * h2:64 * h2 + 64, mc:mc + 128], tgt,
                                        [(ktl[0:16, h * 64:(h + 1) * 64], VB[0:16, h * 128:(h + 1) * 128])],
                                        [ktl, VB], fresh=(m % 4 == 0 and h2 == 0))
                        for m in range(5):
                            tgt = ba if m < 4 else bb
                            mc = (m % 4) * 128
                            E("dve", lambda e, m=m, tgt=tgt, mc=mc, sq=sq, S0=S0, S1=S1: e.scalar_tensor_tensor(
                                out=S1[:, m * 128:(m + 1) * 128], in0=S0[:, m * 128:(m + 1) * 128],
                                scalar=Dsb[:, 4 * m + sq:4 * m + sq + 1], in1=tgt[:, mc:mc + 128],
                                op0=ALU.mult, op1=ALU.add), r=[S0, Dsb, tgt], w=[S1] if m == 0 else (), j=() if m == 0 else [S1])
                        D("sp", lambda e, sq=sq, S1=S1: e.dma_start(out=gla_smp.ap[sq], in_=S1[:]), r=[S1], j=[gla_smp])
                    D("sp", lambda e, mi=mi: e.dma_start(out=g_S[mi].ap, in_=Sbf[:]), r=[Sbf], w=[g_S[mi]])
                if halo and KSEC >= 6:
                    CS = cst_[b2]
                    KR = krot[b2]
                    k4 = kb32[:, :].rearrange("p (h two d) -> p h two d", h=12, two=2)
                    o4 = KR[:, :].rearrange("p (h two d) -> p h two d", h=12, two=2)
                    cosb = CS[:, 0:32].unsqueeze(1).to_broadcast([128, 12, 32])
                    sinb = CS[:, 32:64].unsqueeze(1).to_broadcast([128, 12, 32])
                    r4 = [x[:, :].rearrange("p (h d) -> p h d", h=12) for x in rt]
                    E("pool", lambda e, r4=r4, k4=k4, cosb=cosb: e.tensor_tensor(out=r4[0], in0=k4[:, :, 0, :], in1=cosb, op=ALU.mult), r=[kb32, CS], w=[rt[0]])
                    E("pool", lambda e, r4=r4, k4=k4, sinb=sinb: e.tensor_tensor(out=r4[1], in0=k4[:, :, 1, :], in1=sinb, op=ALU.mult), r=[kb32, CS], w=[rt[1]])
                    E("pool", lambda e, r4=r4, k4=k4, cosb=cosb: e.tensor_tensor(out=r4[2], in0=k4[:, :, 1, :], in1=cosb, op=ALU.mult), r=[kb32, CS], w=[rt[2]])
                    E("pool", lambda e, r4=r4, k4=k4, sinb=sinb: e.tensor_tensor(out=r4[3], in0=k4[:, :, 0, :], in1=sinb, op=ALU.mult), r=[kb32, CS], w=[rt[3]])
                    E("dve", lambda e, r4=r4, o4=o4: e.tensor_tensor(out=o4[:, :, 0, :], in0=r4[0], in1=r4[1], op=ALU.subtract), r=[rt[0], rt[1]], w=[KR])
                    E("dve", lambda e, r4=r4, o4=o4: e.tensor_tensor(out=o4[:, :, 1, :], in0=r4[2], in1=r4[3], op=ALU.add), r=[rt[2], rt[3]], j=[KR])
                    if 24 <= t < 32:
                        D("sp", lambda e, KR=KR, t=t: e.dma_start(out=k_main.ap[(t - 24) * 128:(t - 23) * 128, :], in_=KR[:]),
                          r=[KR], j=[k_main])
                    if smp:
                        for sq in range(4):
                            D("sp", lambda e, KR=KR, sq=sq: e.dma_start(out=ks_out.ap[sq, 2044:2048, :], in_=KR[4 * sq:4 * sq + 4, :]),
                              r=[KR], j=[ks_out])
                    E("act", lambda e, KR=KR: e.copy(out=kbf[:], in_=KR[:]), r=[KR], w=[kbf])
                    bkf = bankbf()
                    transposes([(kbf[:, m * 128:(m + 1) * 128], 128) for m in range(6)], bkf, 128, [kbf], ident)
                    KT_ = KTt[b2]
                    E("act", lambda e, bkf=bkf, KT_=KT_: e.copy(out=KT_[:], in_=bkf[:, 0:768]), r=[bkf], w=[KT_])
                    D("sp", lambda e, KT_=KT_, kb=kb: e.dma_start(out=KTs[kb].ap, in_=KT_[:]), r=[KT_], w=[KTs[kb]])
                    VS = vsb[b2]
                    VA = vaug[b2]
                    E("pool", lambda e, VS=VS, VA=VA: e.tensor_copy(
                        out=VA[:, :, 0:64], in_=VS[:, :].rearrange("p (h d) -> p h d", h=12)), r=[VS], w=[VA])
                    if 24 <= t < 32:
                        D("sp", lambda e, VS=VS, t=t: e.dma_start(out=v_main.ap[(t - 24) * 128:(t - 23) * 128, :], in_=VS[:]),
                          r=[VS], j=[v_main])
                    if smp:
                        for sq in range(4):
                            D("sp", lambda e, VS=VS, sq=sq: e.dma_start(out=vs_out.ap[sq, 2044:2048, :], in_=VS[4 * sq:4 * sq + 4, :]),
                              r=[VS], j=[vs_out])
                    D("sp", lambda e, VA=VA, kb=kb: e.dma_start(out=Vs[kb].ap, in_=VA[:].rearrange("p h d -> p (h d)")),
                      r=[VA], w=[Vs[kb]])
            S.barrier()
            S.replay()
        if stop_after == "A":
            return nc

        def load(eng, dst, src_dt, src_ap=None):
            D(eng, lambda e: e.dma_start(out=dst[:] if not isinstance(dst, tuple) else dst[1],
                                         in_=src_dt.ap if src_ap is None else src_ap),
              r=[src_dt], w=[dst if not isinstance(dst, tuple) else dst[0]])

        with ExitStack() as ph:
            WB = T(nc, ph, "WB", [128, 16, 2688], BF16)
            for (c0, w_, off) in [(0, 640, 0), (2560, 1280, 640), (3856, 768, 1920)]:
                for kq in range(4):
                    D("pool", lambda e, c0=c0, w_=w_, off=off, kq=kq: e.dma_start(
                        out=WB[:, 4 * kq:4 * kq + 4, off:off + w_], in_=win_v[:, 4 * kq:4 * kq + 4, c0:c0 + w_]),
                      j=[WB])
            gnb = T(nc, ph, "gnb", [128, 128], F32)
            D("sp", lambda e: e.dma_start(out=gnb[:], in_=gn_d.to_broadcast([128, 128])), w=[gnb])
            XTb = [T(nc, ph, "XTb%d" % i, [128, 16, 128], BF16) for i in range(2)]
            eqt = [T(nc, ph, "eqt%d" % i, [128, 640], F32) for i in range(2)]
            csb = [T(nc, ph, "csb%d" % i, [128, 64], F32) for i in range(2)]
            qtl = T(nc, ph, "qtl", [128, 640], BF16)
            qTt = [T(nc, ph, "qTt%d" % i, [128, 640], BF16) for i in range(2)]
            rsl = T(nc, ph, "rsl", [128, 1280], F32)
            gat = [T(nc, ph, "gat%d" % i, [128, 1280], BF16) for i in range(2)]
            qb32 = T(nc, ph, "qb32", [128, 768], F32)
            qrot = T(nc, ph, "qrot", [128, 768], F32)
            rtb = [T(nc, ph, "rtb%d" % i, [128, 384], F32) for i in range(4)]
            qbf = T(nc, ph, "qbf", [128, 768], BF16)
            QTt = [T(nc, ph, "QTt%d" % i, [128, 768], BF16) for i in range(2)]
            for mi in range(10):
                t = 23 + mi if mi < 9 else T_SMP
                b2 = mi % 2
                XT = XTb[b2]
                D("sp", lambda e, XT=XT, mi=mi: e.dma_start(
                    out=XT[:], in_=xTs[mi].ap.rearrange("p (a b) -> p a b", a=16)), r=[xTs[mi]], w=[XT])
                EQ = eqt[b2]
                D("sp", lambda e, EQ=EQ, mi=mi: e.dma_start(out=EQ[:], in_=g_eq[mi].ap), r=[g_eq[mi]], w=[EQ])
                CS = csb[b2]
                D("sp", lambda e, CS=CS, t=t: e.dma_start(out=CS[:], in_=cs_d[t * 128:(t + 1) * 128, :]), w=[CS])

                def projB(off, width, XT=XT):
                    res = []
                    c = 0
                    while c < width:
                        n = min(512, width - c)
                        bk = bank()
                        mmgroup(bk[:, 0:n], bk,
                                [(XT[:, kc, :], WB[:, kc, off + c:off + c + n]) for kc in range(16)], [XT, WB])
                        res.append((bk, n, c))
                        c += n
                    return res
                for i, (bk, n, c) in enumerate(projB(0, 640)):
                    E("dve", lambda e, bk=bk, n=n, c=c, EQ=EQ: e.scalar_tensor_tensor(
                        out=qtl[:, c:c + n], in0=bk[:, 0:n], scalar=0.125, in1=EQ[:, c:c + n],
                        op0=ALU.mult, op1=ALU.mult), r=[bk, EQ], w=[qtl] if i == 0 else (), j=() if i == 0 else [qtl])
                bkf = bankbf()
                transposes([(qtl[:, m * 128:(m + 1) * 128], 128) for m in range(5)], bkf, 128, [qtl], ident)
                QA = qTt[b2]
                E("act", lambda e, bkf=bkf, QA=QA: e.copy(out=QA[:], in_=bkf[:, 0:640]), r=[bkf], w=[QA])
                D("sp", lambda e, QA=QA, mi=mi: e.dma_start(out=g_qT[mi].ap, in_=QA[:]), r=[QA], w=[g_qT[mi]])
                for i, (bk, n, c) in enumerate(projB(640, 1280)):
                    E("act", lambda e, bk=bk, n=n, c=c: e.activation(out=rsl[:, c:c + n], in_=bk[:, 0:n], func=AF.Silu),
                      r=[bk], w=[rsl] if i == 0 else (), j=() if i == 0 else [rsl])
                GA = gat[b2]
                E("pool", lambda e, GA=GA: e.tensor_tensor(
                    out=GA[:, :].rearrange("p (h d) -> p h d", h=10), in0=rsl[:, :].rearrange("p (h d) -> p h d", h=10),
                    in1=gnb[:, :].unsqueeze(1).to_broadcast([128, 10, 128]), op=ALU.mult), r=[rsl, gnb], w=[GA])
                D("sp", lambda e, GA=GA, mi=mi: e.dma_start(out=g_gate[mi].ap, in_=GA[:]), r=[GA], w=[g_gate[mi]])
                for i, (bk, n, c) in enumerate(projB(1920, 768)):
                    E("act", lambda e, bk=bk, n=n, c=c: e.copy(out=qb32[:, c:c + n], in_=bk[:, 0:n]),
                      r=[bk], w=[qb32] if i == 0 else (), j=() if i == 0 else [qb32])
                k4 = qb32[:, :].rearrange("p (h two d) -> p h two d", h=12, two=2)
                o4 = qrot[:, :].rearrange("p (h two d) -> p h two d", h=12, two=2)
                cosb = CS[:, 0:32].unsqueeze(1).to_broadcast([128, 12, 32])
                sinb = CS[:, 32:64].unsqueeze(1).to_broadcast([128, 12, 32])
                r4 = [x[:, :].rearrange("p (h d) -> p h d", h=12) for x in rtb]
                E("pool", lambda e, r4=r4, k4=k4, cosb=cosb: e.tensor_tensor(out=r4[0], in0=k4[:, :, 0, :], in1=cosb, op=ALU.mult), r=[qb32, CS], w=[rtb[0]])
                E("pool", lambda e, r4=r4, k4=k4, sinb=sinb: e.tensor_tensor(out=r4[1], in0=k4[:, :, 1, :], in1=sinb, op=ALU.mult), r=[qb32, CS], w=[rtb[1]])
                E("pool", lambda e, r4=r4, k4=k4, cosb=cosb: e.tensor_tensor(out=r4[2], in0=k4[:, :, 1, :], in1=cosb, op=ALU.mult), r=[qb32, CS], w=[rtb[2]])
                E("pool", lambda e, r4=r4, k4=k4, sinb=sinb: e.tensor_tensor(out=r4[3], in0=k4[:, :, 0, :], in1=sinb, op=ALU.mult), r=[qb32, CS], w=[rtb[3]])
                E("dve", lambda e, r4=r4, o4=o4: e.tensor_tensor(out=o4[:, :, 0, :], in0=r4[0], in1=r4[1], op=ALU.subtract), r=[rtb[0], rtb[1]], w=[qrot])
                E("dve", lambda e, r4=r4, o4=o4: e.tensor_tensor(out=o4[:, :, 1, :], in0=r4[2], in1=r4[3], op=ALU.add), r=[rtb[2], rtb[3]], j=[qrot])
                E("act", lambda e: e.copy(out=qbf[:], in_=qrot[:]), r=[qrot], w=[qbf])
                bkf = bankbf()
                transposes([(qbf[:, m * 128:(m + 1) * 128], 128) for m in range(6)], bkf, 128, [qbf], ident)
                QB = QTt[b2]
                E("act", lambda e, bkf=bkf, QB=QB: e.copy(out=QB[:], in_=bkf[:, 0:768]), r=[bkf], w=[QB])
                D("sp", lambda e, QB=QB, mi=mi: e.dma_start(out=g_QT[mi].ap, in_=QB[:]), r=[QB], w=[g_QT[mi]])
            S.barrier()
            S.replay()
        if stop_after == "B":
            return nc

        def gla_out(qT, kT, vv, Ssn, gate, ofull, PT, junkf, ssh, tmpf, maskG_ap, state_terms, osum):
            of4 = ofull[:, 0:1280].rearrange("p (a two b) -> p a two b", two=2, b=128)
            ga4 = gate[:, 0:1280].rearrange("p (a two b) -> p a two b", two=2, b=128)
            firstw = [True]
            for (a0, a1) in ((0, 4), (4, 5)):
                for h2 in range(2):
                    n = a1 - a0
                    heads = [2 * a + h2 for a in range(a0, a1)]
                    Sps = bank()
                    for hl, h in enumerate(heads):
                        m = h // 2
                        for jh in range(2):
                            mmgroup(Sps[64 * jh:64 * jh + 64, hl * 128:(hl + 1) * 128], Sps,
                                    [(kT[64 * h2:64 * h2 + 64, m * 128 + 64 * jh:m * 128 + 64 * jh + 64],
                                      qT[64 * h2:64 * h2 + 64, m * 128:(m + 1) * 128])],
                                    [kT, qT], fresh=(hl == 0 and jh == 0))
                    E("dve", lambda e, Sps=Sps, n=n: e.tensor_tensor(
                        out=PT[:, 0:n * 128].rearrange("p (a b) -> p a b", a=n),
                        in0=Sps[:, 0:n * 128].rearrange("p (a b) -> p a b", a=n),
                        in1=maskG_ap.unsqueeze(1).to_broadcast([128, n, 128]), op=ALU.mult), r=[Sps, gm], w=[PT])
                    Ops = bank()
                    Ops2 = bank()
                    for hl, h in enumerate(heads):
                        mmgroup(Ops[:, hl * 128:(hl + 1) * 128], Ops, [(PT[:, hl * 128:(hl + 1) * 128], vv[:, h * 128:(h + 1) * 128])],
                                [PT, vv], fresh=(hl == 0))
                    first2 = True
                    for hl, h in enumerate(heads):
                        m = h // 2
                        for (r0, r1, pairs, rds) in state_terms(h, m, h2):
                            mmgroup(Ops2[r0:r1, hl * 128:(hl + 1) * 128], Ops2, pairs, rds, fresh=first2)
                            first2 = False
                    E("act", lambda e, Ops2=Ops2, n=n: e.copy(out=tmpf[:, 0:n * 128], in_=Ops2[:, 0:n * 128]), r=[Ops2], w=[tmpf])
                    E("dve", lambda e, Ops=Ops, n=n: e.tensor_tensor(out=osum[:, 0:n * 128], in0=Ops[:, 0:n * 128], in1=tmpf[:, 0:n * 128], op=ALU.add),
                      r=[Ops, tmpf], w=[osum])
                    E("act", lambda e, n=n: e.activation(out=junkf[:, 0:n * 128], in_=osum[:, 0:n * 128], func=AF.Square),
                      r=[osum], w=[junkf])
                    E("dve", lambda e, n=n: e.tensor_reduce(out=ssh[:, 0:n], in_=junkf[:, 0:n * 128].rearrange("p (a b) -> p a b", a=n),
                                                            axis=AX.X, op=ALU.add), r=[junkf], w=[ssh])
                    E("act", lambda e, n=n: e.activation(out=ssh[:, 4:4 + n], in_=ssh[:, 0:n], func=AF.Sqrt, scale=1.0 / 128, bias=EPS),
                      r=[ssh], w=[ssh])
                    E("dve", lambda e, n=n: e.reciprocal(out=ssh[:, 8:8 + n], in_=ssh[:, 4:4 + n]), r=[ssh], w=[ssh])
                    E("dve", lambda e, n=n: e.tensor_tensor(
                        out=tmpf[:, 0:n * 128].rearrange("p (a b) -> p a b", a=n),
                        in0=osum[:, 0:n * 128].rearrange("p (a b) -> p a b", a=n),
                        in1=ssh[:, 8:8 + n].unsqueeze(2).to_broadcast([128, n, 128]), op=ALU.mult), r=[osum, ssh], w=[tmpf])
                    fw = firstw[0]
                    firstw[0] = False
                    E("dve", lambda e, n=n, a0=a0, a1=a1, h2=h2: e.tensor_tensor(
                        out=of4[:, a0:a1, h2, :], in0=tmpf[:, 0:n * 128].rearrange("p (a b) -> p a b", a=n),
                        in1=ga4[:, a0:a1, h2, :], op=ALU.mult),
                      r=[tmpf, gate], w=[ofull] if fw else (), j=() if fw else [ofull])

        def otrans_store(ofull, oTt, mi):
            for q4 in range(2):
                bk = bankbf()
                transposes([(ofull[:, (q4 * 8 + i) * 128:(q4 * 8 + i + 1) * 128], 128) for i in range(8)], bk, 128, [ofull], ident)
                if q4 == 0:
                    E("act", lambda e, bk=bk: e.copy(out=oTt[:, 0:8, :], in_=bk[:, :].rearrange("p (a b) -> p a b", a=8)),
                      r=[bk], w=[oTt])
                else:
                    E("dve", lambda e, bk=bk: e.tensor_copy(out=oTt[:, 8:16, :], in_=bk[:, :].rearrange("p (a b) -> p a b", a=8)),
                      r=[bk], j=[oTt])
            D("sp", lambda e: e.dma_start(out=g_oT[mi].ap.rearrange("p (a b) -> p a b", a=16), in_=oTt[:]), r=[oTt], w=[g_oT[mi]])

        with ExitStack() as ph:
            gmb2 = T(nc, ph, "gmb2", [128, 256], BF16)
            E("dve", lambda e: e.tensor_copy(out=gmb2[:, 0:128], in_=gm[:, 256:384]), r=[gm], w=[gmb2])
            E("dve", lambda e: e.tensor_copy(out=gmb2[:, 128:256], in_=gm[:, 640:768]), r=[gm], j=[gmb2])
            KTa = T(nc, ph, "KTa", [128, 6, NKB * 128], BF16)
            Va = T(nc, ph, "Va", [128, NKB, 780], BF16)
            for kb in range(NKB):
                D("sp", lambda e, kb=kb: e.dma_start(out=KTa[:, :, kb * 128:(kb + 1) * 128],
                                                     in_=KTs[kb].ap.rearrange("p (a b) -> p a b", a=6)), r=[KTs[kb]], j=[KTa])
                D("sp", lambda e, kb=kb: e.dma_start(out=Va[:, kb, :], in_=Vs[kb].ap), r=[Vs[kb]], j=[Va])
            for kb in range(25):
                E("pool", lambda e, kb=kb: e.tensor_scalar(out=Va[:, kb, :], in0=Va[:, kb, :], scalar1=flags[:, kb:kb + 1],
                                                           scalar2=None, op0=ALU.mult), r=[Va, flags], w=[Va])
            maskp = T(nc, ph, "maskp", [128, 17 * 128], BF16)
            D("pool", lambda e: e.dma_start(out=maskp[:], in_=maskp_d), w=[maskp])
            qTc = T(nc, ph, "qTc", [128, 640], BF16)
            kTc = T(nc, ph, "kTc", [128, 640], BF16)
            vc_ = T(nc, ph, "vc", [128, 1280], BF16)
            Ssc = T(nc, ph, "Ssc", [128, 1280], BF16)
            gtc = T(nc, ph, "gtc", [128, 1280], BF16)
            QTc = T(nc, ph, "QTc", [128, 768], BF16)
            ofull = T(nc, ph, "ofull", [128, 2048], BF16)
            PT = T(nc, ph, "PT", [128, 512], BF16)
            junkf = T(nc, ph, "junkf", [128, 512], F32)
            ssh = T(nc, ph, "ssh", [128, 12], F32)
            tmpf = T(nc, ph, "tmpf", [128, 512], F32)
            osum = T(nc, ph, "osum", [128, 512], F32)
            Pts = [T(nc, ph, "Pts%d" % i, [128, 512], BF16) for i in range(3)]
            rden = T(nc, ph, "rden", [128, 12], F32)
            oTt = T(nc, ph, "oTt", [128, 16, 128], BF16)
            pti = [0]
            import os as _os2
            KC = int(_os2.environ.get("KC", "9"))
            KCT = int(_os2.environ.get("KCT", "9"))
            for mi in range(KCT):
                if KC < 2:
                    continue
                for dst, src in ((qTc, g_qT[mi]), (kTc, g_kT[mi]), (vc_, g_v[mi]), (gtc, g_gate[mi]), (QTc, g_QT[mi])):
                    D("sp", lambda e, dst=dst, src=src: e.dma_start(out=dst[:], in_=src.ap), r=[src], w=[dst])
                D("sp", lambda e, mi=mi: e.dma_start(out=Ssc[:], in_=g_S[mi].ap[:, 0:1280]), r=[g_S[mi]], w=[Ssc])

                def st_terms(h, m, h2):
                    return [(0, 64, [(qTc[64 * h2:64 * h2 + 64, m * 128:m * 128 + 64],
                                      Ssc[64 * h2:64 * h2 + 64, m * 128:(m + 1) * 128])], [qTc, Ssc]),
                            (64, 128, [(qTc[64 * h2:64 * h2 + 64, m * 128 + 64:m * 128 + 128],
                                        Ssc[64 * h2:64 * h2 + 64, 640 + m * 128:640 + (m + 1) * 128])], [qTc, Ssc])]
                gla_out(qTc, kTc, vc_, Ssc, gtc, ofull, PT, junkf, ssh, tmpf, gm[:, 256:384], st_terms, osum)
                if KC < 3:
                    continue
                kb0 = mi
                Ob = [pb[4], pb[5]]
                reserved[:] = Ob
                groups = [(h, g) for h in range(12) for g in range(5)]

                def emit_scores(h, g):
                    hp, h2 = h // 2, h % 2
                    i0 = 4 * g
                    n = min(4, 17 - i0)
                    Sps = bank()
                    for ii in range(n):
                        kb = kb0 + i0 + ii
                        for jh in range(2):
                            mmgroup(Sps[64 * jh:64 * jh + 64, ii * 128:(ii + 1) * 128], Sps,
                                    [(KTa[64 * h2:64 * h2 + 64, hp, kb * 128 + 64 * jh:kb * 128 + 64 * jh + 64],
                                      QTc[64 * h2:64 * h2 + 64, hp * 128:(hp + 1) * 128])], [KTa, QTc],
                                    fresh=(ii == 0 and jh == 0))
                    return Sps

                def emit_rest(h, g, Sps):
                    ob = Ob[0] if h < 7 else Ob[1]
                    hr = h if h < 7 else h - 7
                    i0 = 4 * g
                    n = min(4, 17 - i0)
                    Pt = Pts[pti[0] % 3]
                    pti[0] += 1
                    E("act", lambda e, Sps=Sps, Pt=Pt, n=n: e.activation(out=Pt[:, 0:n * 128], in_=Sps[:, 0:n * 128],
                                                                       func=AF.Exp, scale=0.125), r=[Sps], w=[Pt])
                    E("dve", lambda e, Pt=Pt, n=n, i0=i0: e.tensor_tensor(out=Pt[:, 0:n * 128], in0=Pt[:, 0:n * 128],
                                                                         in1=maskp[:, i0 * 128:(i0 + n) * 128], op=ALU.mult),
                      r=[Pt, maskp], w=[Pt])
                    for ii in range(n):
                        i = i0 + ii
                        kb = kb0 + i
                        fn = (lambda e, ob=ob, hr=hr, Pt=Pt, ii=ii, kb=kb, h=h, i=i: e.matmul(
                            ob[:, hr * 65:hr * 65 + 65], lhsT=Pt[:, ii * 128:(ii + 1) * 128],
                            rhs=Va[:, kb, h * 65:(h + 1) * 65], start=(i == 0), stop=(i == 16)))
                        if hr == 0 and i == 0:
                            E("pe", fn, r=[Pt, Va], w=[ob], signal=(ii == n - 1))
                        else:
                            E("pe", fn, r=[Pt, Va], j=[ob], signal=(ii == n - 1))
                pend = emit_scores(*groups[0])
                for gi, (h, g) in enumerate(groups):
                    nxt = emit_scores(*groups[gi + 1]) if gi + 1 < len(groups) else None
                    emit_rest(h, g, pend)
                    pend = nxt
                for (ob, hA, nh) in ((Ob[0], 0, 7), (Ob[1], 7, 5)):
                    ov = ob[:, 0:nh * 65].rearrange("p (a b) -> p a b", a=nh)
                    E("dve", lambda e, ov=ov, hA=hA, nh=nh: e.tensor_scalar(
                        out=rden[:, hA:hA + nh].unsqueeze(2), in0=ov[:, :, 64:65], scalar1=1e-30, scalar2=None, op0=ALU.add),
                      r=[ob], w=[rden] if hA == 0 else (), j=() if hA == 0 else [rden])
                    E("dve", lambda e, hA=hA, nh=nh: e.reciprocal(out=rden[:, hA:hA + nh], in_=rden[:, hA:hA + nh]), r=[rden], w=[rden])
                    E("dve", lambda e, ov=ov, hA=hA, nh=nh: e.tensor_tensor(
                        out=ofull[:, 1280 + hA * 64:1280 + (hA + nh) * 64].rearrange("p (a b) -> p a b", a=nh),
                        in0=ov[:, :, 0:64], in1=rden[:, hA:hA + nh].unsqueeze(2).to_broadcast([128, nh, 64]), op=ALU.mult),
                      r=[ob, rden], j=[ofull])
                reserved[:] = []
                if KC < 4:
                    continue
                otrans_store(ofull, oTt, mi)
            S.barrier()
            S.replay()
        if stop_after == "C":
            return nc

        with ExitStack() as ph:
            gmb2 = T(nc, ph, "gmb2b", [128, 256], BF16)
            E("dve", lambda e: e.tensor_copy(out=gmb2[:, 0:128], in_=gm[:, 256:384]), r=[gm], w=[gmb2])
            E("dve", lambda e: e.tensor_copy(out=gmb2[:, 128:256], in_=gm[:, 640:768]), r=[gm], j=[gmb2])
            masks = T(nc, ph, "masks", [128, 1040], BF16)
            D("pool", lambda e: e.dma_start(out=masks[:], in_=masks_d), w=[masks])
            smb = T(nc, ph, "smb", [128, 64], BF16)
            E("dve", lambda e: e.tensor_copy(out=smb[:], in_=smask[:, 0:64]), r=[smask], w=[smb])
            qTc = T(nc, ph, "qTs", [128, 640], BF16)
            kTc = T(nc, ph, "kTs", [128, 640], BF16)
            vc_ = T(nc, ph, "vs", [128, 1280], BF16)
            Ssc = T(nc, ph, "Sss", [128, 2560], BF16)
            gtc = T(nc, ph, "gts", [128, 1280], BF16)
            QTc = T(nc, ph, "QTs", [128, 768], BF16)
            ofull = T(nc, ph, "ofulls", [128, 2048], BF16)
            PT = T(nc, ph, "PTs", [128, 512], BF16)
            junkf = T(nc, ph, "junkfs", [128, 512], F32)
            ssh = T(nc, ph, "sshs", [128, 12], F32)
            tmpf = T(nc, ph, "tmpfs", [128, 512], F32)
            osum = T(nc, ph, "osums", [128, 512], F32)
            oTt = T(nc, ph, "oTts", [128, 16, 128], BF16)
            qTm = [T(nc, ph, "qTm%d" % i, [128, 5, 128], BF16) for i in range(4)]
            KTn = T(nc, ph, "KTn", [128, 768], BF16)
            Vn = T(nc, ph, "Vn", [128, 780], BF16)
            Kc = T(nc, ph, "Kc", [128, 16, 768], BF16)
            KTc = T(nc, ph, "KTc", [128, 6, 2048], BF16)
            Vc = T(nc, ph, "Vc", [128, 16, 12, 65], BF16)
            Vst = T(nc, ph, "Vst", [128, 16, 768], BF16)
            Pts = [T(nc, ph, "Ptss%d" % i, [128, 64], BF16) for i in range(3)]
            Oacc = T(nc, ph, "Oacc", [16, 780], F32)
            rden = T(nc, ph, "rdens", [16, 12], F32)
            mi = 9
            for dst, src in ((qTc, g_qT[mi]), (kTc, g_kT[mi]), (vc_, g_v[mi]), (gtc, g_gate[mi]), (QTc, g_QT[mi]),
                             (Ssc, g_S[mi]), (KTn, KTs[25]), (Vn, Vs[25])):
                D("sp", lambda e, dst=dst, src=src: e.dma_start(out=dst[:], in_=src.ap), r=[src], w=[dst])
            for sq in range(4):
                E("pool", lambda e, sq=sq: e.memset(qTm[sq][:], 0.0), w=[qTm[sq]])
                E("dve", lambda e, sq=sq: e.tensor_tensor(
                    out=qTm[sq][:, :, 0:16], in0=qTc[:, :].rearrange("p (m t) -> p m t", m=5)[:, :, 0:16],
                    in1=smb[:, sq * 16:(sq + 1) * 16].unsqueeze(1).to_broadcast([128, 5, 16]), op=ALU.mult),
                  r=[qTc, smb], w=[qTm[sq]])

            def st_terms_s(h, m, h2):
                return [(64 * jh, 64 * jh + 64,
                         [(qTm[sq][64 * h2:64 * h2 + 64, m, 64 * jh:64 * jh + 64],
                           Ssc[64 * h2:64 * h2 + 64, sq * 640 + m * 128:sq * 640 + (m + 1) * 128]) for sq in range(4)],
                         qTm + [Ssc]) for jh in range(2)]
            E("pool", lambda e: e.memset(ofull[:, 1280:2048], 0.0), w=[ofull])
            gla_out(qTc, kTc, vc_, Ssc, gtc, ofull, PT, junkf, ssh, tmpf, gm[:, 640:768], st_terms_s, osum)
            E("pool", lambda e: e.memset(Vc[:], 1.0), w=[Vc])
            pti = [0]
            for sq in range(4):
                for b4 in range(4):
                    D("pool", lambda e, sq=sq, b4=b4: e.dma_start(
                        out=Kc[:, 4 * b4:4 * b4 + 4, :],
                        in_=ck_d[sq, 512 * b4:512 * b4 + 512, :].rearrange("(b p) c -> p b c", p=128)),
                      w=[Kc] if b4 == 0 else (), j=() if b4 == 0 else [Kc])
                for b4 in range(4):
                    D("pool", lambda e, sq=sq, b4=b4: e.dma_start(
                        out=Vst[:, 4 * b4:4 * b4 + 4, :],
                        in_=cv_d[sq, 512 * b4:512 * b4 + 512, :].rearrange("(b p) c -> p b c", p=128)),
                      w=[Vst] if b4 == 0 else (), j=() if b4 == 0 else [Vst])
                for b4 in range(4):
                    E("dve" if b4 % 2 else "pool", lambda e, b4=b4: e.tensor_copy(
                        out=Vc[:, 4 * b4:4 * b4 + 4, :, 0:64],
                        in_=Vst[:, 4 * b4:4 * b4 + 4, :].rearrange("p b (h d) -> p b h d", h=12)),
                      r=[Vst], w=[Vc] if b4 == 0 else (), j=() if b4 == 0 else [Vc])
                for hp in range(6):
                    for b8 in range(2):
                        bkf = bankbf()
                        transposes([(Kc[:, b8 * 8 + i, hp * 128:(hp + 1) * 128], 128) for i in range(8)], bkf, 128, [Kc], ident)
                        E("act" if b8 == 0 else "dve",
                          (lambda e, bkf=bkf, hp=hp, b8=b8: e.copy(out=KTc[:, hp, b8 * 1024:(b8 + 1) * 1024], in_=bkf[:, :]))
                          if b8 == 0 else
                          (lambda e, bkf=bkf, hp=hp, b8=b8: e.tensor_copy(out=KTc[:, hp, b8 * 1024:(b8 + 1) * 1024], in_=bkf[:, :])),
                          r=[bkf], w=[KTc] if (hp == 0 and b8 == 0) else (), j=() if (hp == 0 and b8 == 0) else [KTc])
                Ob = [pb[4], pb[5]]
                reserved[:] = Ob
                for h in range(12):
                    hp, h2 = h // 2, h % 2
                    ob = Ob[0] if h < 7 else Ob[1]
                    hr = h if h < 7 else h - 7
                    ngrp = 5 if sq == 0 else 4
                    for g in range(ngrp):
                        Sps = bank()
                        if g < 4:
                            for ii in range(4):
                                blk = 4 * g + ii
                                for jh in range(2):
                                    mmgroup(Sps[64 * jh:64 * jh + 64, ii * 16:(ii + 1) * 16], Sps,
                                            [(KTc[64 * h2:64 * h2 + 64, hp, blk * 128 + 64 * jh:blk * 128 + 64 * jh + 64],
                                              QTc[64 * h2:64 * h2 + 64, hp * 128:hp * 128 + 16])], [KTc, QTc],
                                            fresh=(ii == 0 and jh == 0))
                            ncol, moff = 64, (sq * 16 + 4 * g) * 16
                        else:
                            for jh in range(2):
                                mmgroup(Sps[64 * jh:64 * jh + 64, 0:16], Sps,
                                        [(KTn[64 * h2:64 * h2 + 64, hp * 128 + 64 * jh:hp * 128 + 64 * jh + 64],
                                          QTc[64 * h2:64 * h2 + 64, hp * 128:hp * 128 + 16])], [KTn, QTc], fresh=(jh == 0))
                            ncol, moff = 16, 1024
                        Pt = Pts[pti[0] % 3]
                        pti[0] += 1
                        E("act", lambda e, Sps=Sps, Pt=Pt, ncol=ncol: e.activation(out=Pt[:, 0:ncol], in_=Sps[:, 0:ncol],
                                                                                 func=AF.Exp, scale=0.125), r=[Sps], w=[Pt])
                        E("dve", lambda e, Pt=Pt, ncol=ncol, moff=moff: e.tensor_tensor(
                            out=Pt[:, 0:ncol], in0=Pt[:, 0:ncol], in1=masks[:, moff:moff + ncol], op=ALU.mult),
                          r=[Pt, masks], w=[Pt])
                        nblk = 4 if g < 4 else 1
                        for ii in range(nblk):
                            first = (g == 0 and ii == 0)
                            last = (g == ngrp - 1 and ii == nblk - 1)
                            if g < 4:
                                blk = 4 * g + ii
                                fn = (lambda e, ob=ob, hr=hr, Pt=Pt, ii=ii, blk=blk, h=h, first=first, last=last: e.matmul(
                                    ob[0:16, hr * 65:hr * 65 + 65], lhsT=Pt[:, ii * 16:(ii + 1) * 16],
                                    rhs=Vc[:, blk, h, :], start=first, stop=last))
                                rd = [Pt, Vc]
                            else:
                                fn = (lambda e, ob=ob, hr=hr, Pt=Pt, h=h, first=first, last=last: e.matmul(
                                    ob[0:16, hr * 65:hr * 65 + 65], lhsT=Pt[:, 0:16],
                                    rhs=Vn[:, h * 65:(h + 1) * 65], start=first, stop=last))
                                rd = [Pt, Vn]
                            if hr == 0 and first:
                                E("pe", fn, r=rd, w=[ob], signal=True)
                            else:
                                E("pe", fn, r=rd, j=[ob], signal=True)
                for (ob, hA, nh) in ((Ob[0], 0, 7), (Ob[1], 7, 5)):
                    if sq == 0:
                        E("dve", lambda e, ob=ob, hA=hA, nh=nh: e.tensor_copy(out=Oacc[:, hA * 65:(hA + nh) * 65], in_=ob[0:16, 0:nh * 65]),
                          r=[ob], w=[Oacc] if hA == 0 else (), j=() if hA == 0 else [Oacc])
                    else:
                        E("dve", lambda e, ob=ob, hA=hA, nh=nh: e.tensor_tensor(
                            out=Oacc[:, hA * 65:(hA + nh) * 65], in0=Oacc[:, hA * 65:(hA + nh) * 65], in1=ob[0:16, 0:nh * 65], op=ALU.add),
                          r=[ob, Oacc], w=[Oacc])
            reserved[:] = []
            ov = Oacc[:, :].rearrange("p (a b) -> p a b", a=12)
            E("dve", lambda e: e.tensor_scalar(out=rden[:, :].unsqueeze(2), in0=ov[:, :, 64:65], scalar1=1e-30, scalar2=None, op0=ALU.add),
              r=[Oacc], w=[rden])
            E("dve", lambda e: e.reciprocal(out=rden[:], in_=rden[:]), r=[rden], w=[rden])
            E("dve", lambda e: e.tensor_tensor(
                out=ofull[0:16, 1280:2048].rearrange("p (a b) -> p a b", a=12), in0=ov[:, :, 0:64],
                in1=rden[:, :].unsqueeze(2).to_broadcast([16, 12, 64]), op=ALU.mult), r=[Oacc, rden], j=[ofull])
            otrans_store(ofull, oTt, mi)
            S.barrier()
            S.replay()
        if stop_after == "C2":
            return nc

        with ExitStack() as phDG:
            hnT = T(nc, phDG, "hnT", [128, 16, 1042], BF16)
            with ExitStack() as ph:
                wout_v = wout_d.rearrange("(kc p) c -> p kc c", p=128)
                WO = T(nc, ph, "WO", [128, 16, 2048], BF16)
                for kq in range(4):
                    D("pool", lambda e, kq=kq: e.dma_start(out=WO[:, 4 * kq:4 * kq + 4, :], in_=wout_v[:, 4 * kq:4 * kq + 4, :]), j=[WO])
                gffn = T(nc, ph, "gffn", [128, 2048], F32)
                D("sp", lambda e: e.dma_start(out=gffn[:], in_=nffn_d.to_broadcast([128, 2048])), w=[gffn])
                oTd = [T(nc, ph, "oTd%d" % i, [128, 16, 128], BF16) for i in range(2)]
                Xd = [T(nc, ph, "Xd%d" % i, [128, 2048], F32) for i in range(2)]
                h1t = [T(nc, ph, "h1t%d" % i, [128, 2048], F32) for i in range(2)]
                hnb = T(nc, ph, "hnb", [128, 2048], BF16)
                ssd = [T(nc, ph, "ssd%d" % i, [128, 2], F32) for i in range(2)]
                rsd = [T(nc, ph, "rsd%d" % i, [128, 1], F32) for i in range(2)]
                for mi in range(10):
                    t = 23 + mi if mi < 9 else T_SMP
                    b2 = mi % 2
                    OT, X, H1 = oTd[b2], Xd[b2], h1t[b2]
                    D("sp", lambda e, OT=OT, mi=mi: e.dma_start(out=OT[:], in_=g_oT[mi].ap.rearrange("p (a b) -> p a b", a=16)),
                      r=[g_oT[mi]], w=[OT])
                    D("sp", lambda e, X=X, t=t: e.dma_start(out=X[:], in_=xall[t * 128:(t + 1) * 128, :]), w=[X])
                    for n in range(4):
                        bk = bank()
                        mmgroup(bk[:, 0:512], bk, [(OT[:, kc, :], WO[:, kc, n * 512:(n + 1) * 512]) for kc in range(16)], [OT, WO])
                        E("dve", lambda e, bk=bk, n=n, X=X, H1=H1: e.tensor_tensor(
                            out=H1[:, n * 512:(n + 1) * 512], in0=bk[:, 0:512], in1=X[:, n * 512:(n + 1) * 512], op=ALU.add),
                          r=[bk, X], w=[H1] if n == 0 else (), j=() if n == 0 else [H1])
                    D("sp", lambda e, H1=H1, mi=mi: e.dma_start(out=g_h1[mi].ap, in_=H1[:]), r=[H1], w=[g_h1[mi]])
                    rms_rstd(H1[:], hnb, ssd[b2], rsd[b2], H1, 2048)
                    E("dve", lambda e, H1=H1, b2=b2: e.scalar_tensor_tensor(
                        out=hnb[:], in0=H1[:], scalar=rsd[b2][:, 0:1], in1=gffn[:], op0=ALU.mult, op1=ALU.mult),
                      r=[H1, rsd[b2], gffn], w=[hnb])
                    if mi == 0:
                        E("dve", lambda e: e.tensor_scalar(out=hnb[:], in0=hnb[:], scalar1=flags[:, 26:27], scalar2=None, op0=ALU.mult),
                          r=[hnb, flags], w=[hnb])
                    if mi == 0:
                        src0, ncol, dst0 = 126, 2, 0
                    elif mi < 9:
                        src0, ncol, dst0 = 0, 128, 2 + 128 * (mi - 1)
                    else:
                        src0, ncol, dst0 = 0, 16, 1026
                    for q4 in range(2):
                        bk = bankbf()
                        transposes([(hnb[:, (q4 * 8 + i) * 128:(q4 * 8 + i + 1) * 128], 128) for i in range(8)], bk, 128, [hnb], ident)
                        fnc = (lambda e, bk=bk, q4=q4, src0=src0, ncol=ncol, dst0=dst0: (e.copy if q4 == 0 else e.tensor_copy)(
                            out=hnT[:, q4 * 8:q4 * 8 + 8, dst0:dst0 + ncol],
                            in_=bk[:, :].rearrange("p (a b) -> p a b", a=8)[:, :, src0:src0 + ncol]))
                        E("act" if q4 == 0 else "dve", fnc, r=[bk], j=[hnT])
                S.barrier()
                S.replay()
            if stop_after == "D":
                return nc

            actT = T(nc, phDG, "actT", [128, NF, 1040], BF16)
            with ExitStack() as ph:
                wup_v = wup_d.rearrange("(kc p) c -> p kc c", p=128)
                wu = [T(nc, ph, "wu%d" % i, [128, 16, 256], BF16) for i in range(3)]
                tu = [T(nc, ph, "tu%d" % i, [128, 512], F32) for i in range(2)]
                tg = [T(nc, ph, "tg%d" % i, [128, 512], F32) for i in range(2)]
                sgt = [T(nc, ph, "sgt%d" % i, [128, 512], F32) for i in range(2)]
                Eb = [T(nc, ph, "Eb%d" % i, [128, 4, 6], F32) for i in range(2)]
                tS = [T(nc, ph, "tS%d" % i, [128, 16], F32) for i in range(2)]
                sgS = T(nc, ph, "sgS", [128, 16], F32)
                upl = [T(nc, ph, "upl%d" % i, [128, 18], F32) for i in range(2)]
                ups = [T(nc, ph, "ups%d" % i, [18, 128], F32) for i in range(2)]
                chunks = [(0, 512, 0), (510, 512, 510), (1020, 22, 1020)]
                cnt = [0]
                for f in range(NF):
                    W = wu[f % 3]
                    for kq in range(4):
                        D("pool", lambda e, W=W, f=f, kq=kq: e.dma_start(
                            out=W[:, 4 * kq:4 * kq + 4, 0:128], in_=wup_v[:, 4 * kq:4 * kq + 4, f * 128:(f + 1) * 128]),
                          w=[W] if kq == 0 else (), j=() if kq == 0 else [W])
                        D("pool", lambda e, W=W, f=f, kq=kq: e.dma_start(
                            out=W[:, 4 * kq:4 * kq + 4, 128:256],
                            in_=wup_v[:, 4 * kq:4 * kq + 4, D_FF + f * 128:D_FF + (f + 1) * 128]), j=[W])
                    for (c0, n, a0) in chunks:
                        k = cnt[0] % 2
                        cnt[0] += 1
                        TU, TG, SG = tu[k], tg[k], sgt[k]
                        res = []
                        for half, tt in ((0, TU), (1, TG)):
                            fi = f + NF * half
                            bk = bank()
                            mmgroup(bk[:, 0:n], bk, [(W[:, kc, half * 128:(half + 1) * 128], hnT[:, kc, c0:c0 + n]) for kc in range(16)],
                                    [W, hnT])
                            npr = (n - 2) if n == 512 else 4
                            E("act", lambda e, bk=bk, tt=tt, fi=fi, npr=npr: e.activation(
                                out=tt[:, 0:npr], in_=bk[:, 0:npr], func=AF.Identity, scale=cwT[:, fi, 0:1], bias=cwT[:, fi, 3:4]),
                              r=[bk, cwT], w=[tt])
                            E("dve", lambda e, bk=bk, tt=tt, fi=fi, npr=npr: e.scalar_tensor_tensor(
                                out=tt[:, 0:npr], in0=bk[:, 1:1 + npr], scalar=cwT[:, fi, 1:2], in1=tt[:, 0:npr], op0=ALU.mult, op1=ALU.add),
                              r=[bk, cwT, tt], w=[tt])
                            E("dve", lambda e, bk=bk, tt=tt, fi=fi, npr=npr: e.scalar_tensor_tensor(
                                out=tt[:, 0:npr], in0=bk[:, 2:2 + npr], scalar=cwT[:, fi, 2:3], in1=tt[:, 0:npr], op0=ALU.mult, op1=ALU.add),
                              r=[bk, cwT, tt], w=[tt])
                            if n != 512:
                                EB, TS, UL, UP = Eb[half], tS[half], upl[half], ups[half]
                                E("dve", lambda e, EB=EB, fi=fi: e.tensor_copy(
                                    out=EB[:, :, 0:2], in_=cwT[:, fi, 4:12].rearrange("p (s r) -> p s r", s=4)), r=[cwT], w=[EB])
                                E("dve", lambda e, EB=EB, bk=bk: e.tensor_copy(
                                    out=EB[:, :, 2:6], in_=bk[:, 6:22].rearrange("p (s r) -> p s r", s=4)), r=[bk], j=[EB])
                                E("dve", lambda e, UL=UL, bk=bk: e.tensor_copy(out=UL[:, 0:18], in_=bk[:, 4:22]), r=[bk], w=[UL])
                                ts3 = TS[:, :].rearrange("p (s r) -> p s r", s=4)
                                E("act", lambda e, EB=EB, ts3=ts3, fi=fi: e.activation(
                                    out=ts3, in_=EB[:, :, 0:4], func=AF.Identity, scale=cwT[:, fi, 0:1], bias=cwT[:, fi, 3:4]),
                                  r=[EB, cwT], w=[TS])
                                E("dve", lambda e, EB=EB, ts3=ts3, fi=fi: e.scalar_tensor_tensor(
                                    out=ts3, in0=EB[:, :, 1:5], scalar=cwT[:, fi, 1:2], in1=ts3, op0=ALU.mult, op1=ALU.add),
                                  r=[EB, cwT, TS], w=[TS])
                                E("dve", lambda e, EB=EB, ts3=ts3, fi=fi: e.scalar_tensor_tensor(
                                    out=ts3, in0=EB[:, :, 2:6], scalar=cwT[:, fi, 2:3], in1=ts3, op0=ALU.mult, op1=ALU.add),
                                  r=[EB, cwT, TS], w=[TS])
                                bkt = bank()
                                E("pe", lambda e, bkt=bkt, UL=UL: e.transpose(out=bkt[0:18, 0:128], in_=UL[:, 0:18], identity=identf[:]),
                                  r=[UL, identf], w=[bkt])
                                E("act", lambda e, bkt=bkt, UP=UP: e.copy(out=UP[:], in_=bkt[0:18, 0:128]), r=[bkt], w=[UP])
                                D("sp", lambda e, UP=UP, fi=fi: e.dma_start(out=conv_dev.ap[:, fi * 128:(fi + 1) * 128], in_=UP[:]),
                                  r=[UP], j=[conv_dev])
                            res.append(npr)
                        npr = res[0]
                        E("act", lambda e, TG=TG, SG=SG, npr=npr: e.activation(out=SG[:, 0:npr], in_=TG[:, 0:npr], func=AF.Silu),
                          r=[TG], w=[SG])
                        E("pool", lambda e, TU=TU, SG=SG, npr=npr, f=f, a0=a0: e.tensor_tensor(
                            out=actT[:, f, a0:a0 + npr], in0=SG[:, 0:npr], in1=TU[:, 0:npr], op=ALU.mult), r=[SG, TU], j=[actT])
                        if n != 512:
                            E("act", lambda e: e.activation(out=sgS[:], in_=tS[1][:], func=AF.Silu), r=[tS[1]], w=[sgS])
                            E("pool", lambda e, f=f: e.tensor_tensor(out=actT[:, f, 1024:1040], in0=sgS[:], in1=tS[0][:], op=ALU.mult),
                              r=[sgS, tS[0]], j=[actT])
                S.barrier()
                S.replay()
            if stop_after == "E":
                return nc
            with ExitStack() as ph:
                wdn_v = wdn_d.rearrange("(f p) c -> p f c", p=128)
                wd = [T(nc, ph, "wd%d" % i, [128, NF, 256], BF16) for i in range(2)]
                h1s = [T(nc, ph, "h1s%d" % i, [128, 256], F32) for i in range(3)]
                h2s = [T(nc, ph, "h2s%d" % i, [128, 256], F32) for i in range(3)]
                cnt = [0]
                for ng in range(8):
                    W = wd[ng % 2]
                    for fq in range(0, NF, 4):
                        f1 = min(NF, fq + 4)
                        D("pool", lambda e, W=W, ng=ng, fq=fq, f1=f1: e.dma_start(
                            out=W[:, fq:f1, :], in_=wdn_v[:, fq:f1, ng * 256:(ng + 1) * 256]),
                          w=[W] if fq == 0 else (), j=() if fq == 0 else [W])
                    for j in range(9):
                        mi = j + 1
                        M = 128 if j < 8 else 16
                        a0 = 128 * j if j < 8 else 1024
                        k = cnt[0] % 3
                        cnt[0] += 1
                        H1, H2 = h1s[k], h2s[k]
                        D("sp", lambda e, H1=H1, mi=mi, ng=ng: e.dma_start(out=H1[:], in_=g_h1[mi].ap[:, ng * 256:(ng + 1) * 256]),
                          r=[g_h1[mi]], w=[H1])
                        bk = bank()
                        mmgroup(bk[0:M, 0:256], bk, [(actT[:, f, a0:a0 + M], W[:, f, :]) for f in range(NF)], [actT, W])
                        E("dve", lambda e, bk=bk, H1=H1, H2=H2, M=M: e.tensor_tensor(out=H2[0:M, :], in0=bk[0:M, 0:256], in1=H1[0:M, :], op=ALU.add),
                          r=[bk, H1], w=[H2])
                        D("sp", lambda e, H2=H2, mi=mi, ng=ng, M=M: e.dma_start(out=g_h2[mi].ap[0:M, ng * 256:(ng + 1) * 256], in_=H2[0:M, :]),
                          r=[H2], j=[g_h2[mi]])
                S.barrier()
                S.replay()
            if stop_after == "F":
                return nc
            with ExitStack() as ph:
                gfin = T(nc, ph, "gfin", [128, 2048], F32)
                D("sp", lambda e: e.dma_start(out=gfin[:], in_=nfin_d.to_broadcast([128, 2048])), w=[gfin])
                h2t = [T(nc, ph, "h2t%d" % i, [128, 2048], F32) for i in range(2)]
                yt = [T(nc, ph, "yt%d" % i, [128, 2048], F32) for i in range(2)]
                jk = T(nc, ph, "jk", [128, 2048], BF16)
                ssg = [T(nc, ph, "ssg%d" % i, [128, 2], F32) for i in range(2)]
                rsg = [T(nc, ph, "rsg%d" % i, [128, 1], F32) for i in range(2)]
                for j in range(9):
                    mi = j + 1
                    M = 128 if j < 8 else 16
                    b2 = j % 2
                    H2, Y = h2t[b2], yt[b2]
                    D("sp", lambda e, H2=H2, mi=mi, M=M: e.dma_start(out=H2[0:M, :], in_=g_h2[mi].ap[0:M, :]), r=[g_h2[mi]], w=[H2])
                    E("act", lambda e, H2=H2, b2=b2, M=M: e.activation(out=jk[0:M, :], in_=H2[0:M, :], func=AF.Square, accum_out=ssg[b2][0:M, 0:1]),
                      r=[H2], w=[jk, ssg[b2]])
                    E("act", lambda e, b2=b2, M=M: e.activation(out=ssg[b2][0:M, 1:2], in_=ssg[b2][0:M, 0:1], func=AF.Sqrt, scale=1.0 / 2048, bias=EPS),
                      r=[ssg[b2]], w=[ssg[b2]])
                    E("dve", lambda e, b2=b2, M=M: e.reciprocal(out=rsg[b2][0:M, 0:1], in_=ssg[b2][0:M, 1:2]), r=[ssg[b2]], w=[rsg[b2]])
                    E("dve", lambda e, H2=H2, Y=Y, b2=b2, M=M: e.scalar_tensor_tensor(
                        out=Y[0:M, :], in0=H2[0:M, :], scalar=rsg[b2][0:M, 0:1], in1=gfin[0:M, :], op0=ALU.mult, op1=ALU.mult),
                      r=[H2, rsg[b2], gfin], w=[Y])
                    if j < 8:
                        D("sp", lambda e, Y=Y, j=j: e.dma_start(out=y_main.ap[j * 128:(j + 1) * 128, :], in_=Y[:]), r=[Y], j=[y_main])
                    else:
                        D("sp", lambda e, Y=Y: e.dma_start(out=y_smp.ap, in_=Y[0:16, :]), r=[Y], w=[y_smp])
                S.barrier()
                S.replay()
        return nc


_CACHE = {}


def _get_nc(stop_after="Z"):
    if stop_after not in _CACHE:
        _CACHE[stop_after] = build(stop_after)
    return _CACHE[stop_after]


def kernel(x_prompt, x_sample, state_gla, cache_dil_k, cache_dil_v, state_ffn_conv,
           norm_mix, w_in, w_gate_up, b_gate, gla_norm, w_out,
           norm_ffn, w_ffn_up, ffn_conv_w, ffn_conv_b, w_ffn_down, norm_final, _stop_after="Z"):
    f = lambda a: np.ascontiguousarray(np.asarray(a, dtype=np.float32))
    x_prompt, x_sample = f(x_prompt), f(x_sample)
    state_gla, ck, cv, sconv = f(state_gla)[0], f(cache_dil_k)[0], f(cache_dil_v)[0], f(state_ffn_conv)[0]
    shared = dict(norm_mix=f(norm_mix).reshape(1, 2048), w_in=f(w_in)[0], wg=f(w_gate_up)[0],
                  bg=f(b_gate).reshape(1, 640), gnorm=f(gla_norm).reshape(1, 128), w_out=f(w_out)[0],
                  norm_ffn=f(norm_ffn).reshape(1, 2048), w_up=f(w_ffn_up)[0], w_down=f(w_ffn_down)[0],
                  norm_final=f(norm_final).reshape(1, 2048))
    convw = f(ffn_conv_w)[0]
    convb = f(ffn_conv_b).reshape(1, 11008)
    in_maps = []
    for c in range(8):
        b, s = c // 4, c % 4
        start = 1024 * s
        xall = np.zeros((NT * 128, 2048), np.float32)
        lo = start - 3072
        src_lo = max(lo, 0)
        xall[src_lo - lo:4096] = x_prompt[b, src_lo:start + 1024]
        xall[T_SMP * 128:T_SMP * 128 + 16] = x_sample[4 * c:4 * c + 4].reshape(16, 2048)
        m = dict(shared)
        m.update(_host_consts(c))
        m["xall"] = xall
        m["sgla"] = np.ascontiguousarray(state_gla[4 * c:4 * c + 4].reshape(4, 640, 128))
        m["ck"] = np.ascontiguousarray(ck[4 * c:4 * c + 4].reshape(4, 2048, 768))
        m["cv"] = np.ascontiguousarray(cv[4 * c:4 * c + 4].reshape(4, 2048, 768))
        m["convp"] = np.ascontiguousarray(np.concatenate([convw, convb, sconv[4 * c:4 * c + 4].reshape(8, 11008)], axis=0))
        in_maps.append(m)
    nc = _get_nc(_stop_after)
    if _stop_after in ("P", "A"):
        for m in in_maps:
            m.pop("w_up"); m.pop("w_down")
    res = run_bass_kernel_spmd(nc, in_maps, core_ids=list(range(8)))
    R = res.results
    y_prompt = np.zeros((2, 4096, 2048), np.float32)
    y_sample = np.zeros((32, 4, 2048), np.float32)
    gla_p = np.zeros((1, 2, 10, 64, 128), np.float32)
    gla_s = np.zeros((1, 32, 10, 64, 128), np.float32)
    k_p = np.zeros((1, 2, 2048, 12, 64), np.float32)
    v_p = np.zeros((1, 2, 2048, 12, 64), np.float32)
    k_s = np.zeros((1, 32, 2048, 12, 64), np.float32)
    v_s = np.zeros((1, 32, 2048, 12, 64), np.float32)
    c_p = np.zeros((1, 2, 2, 11008), np.float32)
    c_s = np.zeros((1, 32, 2, 11008), np.float32)

    def unstate(a):
        return a.reshape(2, 64, 5, 128).transpose(2, 0, 1, 3).reshape(10, 64, 128)
    for c in range(8):
        b, s = c // 4, c % 4
        r = R[c]
        y_prompt[b, 1024 * s:1024 * s + 1024] = r["y_main"]
        y_sample[4 * c:4 * c + 4] = r["y_smp"].reshape(4, 4, 2048)
        if s == 3:
            gla_p[0, b] = unstate(r["gla_end"])
            c_p[0, b] = r["conv_dev"][0:2]
        for sq in range(4):
            gla_s[0, 4 * c + sq] = unstate(r["gla_smp"][sq])
            k_s[0, 4 * c + sq] = r["ks_out"][sq].reshape(2048, 12, 64)
            v_s[0, 4 * c + sq] = r["vs_out"][sq].reshape(2048, 12, 64)
            c_s[0, 4 * c + sq] = r["conv_dev"][2 + 4 * sq + 2:2 + 4 * sq + 4]
        if s >= 2:
            k_p[0, b, (s - 2) * 1024:(s - 1) * 1024] = r["k_main"].reshape(1024, 12, 64)
            v_p[0, b, (s - 2) * 1024:(s - 1) * 1024] = r["v_main"].reshape(1024, 12, 64)
    return (y_prompt, y_sample, gla_p, gla_s, k_p, k_s, v_p, v_s, c_p, c_s)
```
